# Optimizing a Trainium2 kernel written in Bass

```python
import math
import jax, jax.numpy as jnp
from jax import lax
import numpy as np

D_MODEL = 1024
BATCH = 2
SEQ = 16384
DEPTH = 2

N_MIXERS = 2
EPS = 1e-6

CONV_WIDTH = 2 * D_MODEL
CONV_K = 3

HEAD_DIM = D_MODEL // 16
DIL_PAIRS = ((128, 1), (512, 4), (2048, 16))
N_GROUPS = len(DIL_PAIRS)
ATTN_WIDTH = (3 * D_MODEL) // 2
HEADS_PER_GROUP = ATTN_WIDTH // (N_GROUPS * HEAD_DIM)
N_HEADS = N_GROUPS * HEADS_PER_GROUP
Q_BLOCK = 128

REL_BUCKETS = 32
REL_MAX_DIST = 1024

kernel_name = "hybrid_shortconv_dilated_attn_encoder"


def rmsnorm(x, g):
    xf = x.astype(jnp.float32)
    y = xf * lax.rsqrt(jnp.mean(xf * xf, axis=-1, keepdims=True) + EPS)
    return (y * g.astype(jnp.float32)).astype(x.dtype)


def t5_bucket(rel):
    nb = REL_BUCKETS // 2
    ret = (rel > 0).astype(np.int32) * nb
    n = np.abs(rel)
    max_exact = nb // 2
    large = max_exact + (np.log(np.maximum(n, 1) / max_exact)
                         / np.log(REL_MAX_DIST / max_exact) * (nb - max_exact)).astype(np.int32)
    large = np.minimum(large, nb - 1)
    return ret + np.where(n < max_exact, n, large).astype(np.int32)


def short_conv_mixer(hn, w_in, conv_w, conv_b, w_out):
    proj = hn @ w_in
    b_gate, c_gate, u, z = jnp.split(proj, 4, axis=-1)
    cu = lax.conv_general_dilated(
        c_gate * u, conv_w[:, None, :].astype(u.dtype),
        window_strides=(1,), padding=((1, 1),),
        dimension_numbers=('NWC', 'WIO', 'NWC'),
        feature_group_count=CONV_WIDTH) + conv_b
    y = b_gate * cu * jax.nn.silu(z)
    return y @ w_out


def dilated_attention_mixer(hn, w_in, q_gain, k_gain, rel_table, w_out):
    bsz, s, _ = hn.shape
    proj = hn @ w_in
    qkv = proj[..., :3 * ATTN_WIDTH].reshape(bsz, s, N_GROUPS, 3, HEADS_PER_GROUP, HEAD_DIM)
    z = proj[..., 3 * ATTN_WIDTH:]
    gshape = (N_GROUPS, HEADS_PER_GROUP, HEAD_DIM)
    q = rmsnorm(qkv[:, :, :, 0], q_gain.reshape(gshape)) * (HEAD_DIM ** -0.5)
    k = rmsnorm(qkv[:, :, :, 1], k_gain.reshape(gshape))
    v = qkv[:, :, :, 2]
    n_blk = s // Q_BLOCK
    starts = jnp.arange(n_blk, dtype=jnp.int32) * Q_BLOCK

    outs, lses = [], []
    for g, (window, dil) in enumerate(DIL_PAIRS):
        half = (window // 2) // dil
        offs_np = (np.arange(-half, half + 1) * dil).astype(np.int32)
        bias = rel_table[t5_bucket(offs_np)][:, g * HEADS_PER_GROUP:(g + 1) * HEADS_PER_GROUP]
        bias = bias.astype(jnp.float32).T
        offs = jnp.asarray(offs_np)
        qg, kg, vg = q[:, :, g], k[:, :, g], v[:, :, g]

        def block(start, qg=qg, kg=kg, vg=vg, offs=offs, bias=bias):
            qb = lax.dynamic_slice_in_dim(qg, start, Q_BLOCK, axis=1)
            kpos = start + jnp.arange(Q_BLOCK, dtype=jnp.int32)[:, None] + offs[None, :]
            valid = (kpos >= 0) & (kpos < s)
            kidx = jnp.clip(kpos, 0, s - 1)
            kb = jnp.take(kg, kidx, axis=1)
            vb = jnp.take(vg, kidx, axis=1)
            logits = jnp.einsum('bqhd,bqkhd->bqhk', qb, kb).astype(jnp.float32) + bias[None, None]
            logits = jnp.where(valid[None, :, None, :], logits, -jnp.inf)
            lse = jax.nn.logsumexp(logits, axis=-1)
            p = jnp.exp(logits - lse[..., None])
            o = jnp.einsum('bqhk,bqkhd->bqhd', p.astype(vb.dtype), vb)
            return o, lse

        o, lse = lax.map(block, starts)
        outs.append(jnp.moveaxis(o, 0, 1).reshape(bsz, s, HEADS_PER_GROUP, HEAD_DIM))
        lses.append(jnp.moveaxis(lse, 0, 1).reshape(bsz, s, HEADS_PER_GROUP))

    o_all = jnp.stack(outs, axis=2)
    alpha = jax.nn.softmax(jnp.stack(lses, axis=2), axis=2)
    y = (o_all * alpha[..., None].astype(o_all.dtype)).reshape(bsz, s, ATTN_WIDTH)
    y = y * jax.nn.silu(z)
    return y @ w_out


def setup_inputs(seed: int = 0) -> dict:
    key = jax.random.key(seed)
    ks = jax.random.split(key, 12)
    n_a = (DEPTH + 1) // 2
    n_b = DEPTH // 2
    f32 = jnp.float32
    x = jax.random.normal(ks[0], (BATCH, SEQ, D_MODEL), f32)
    norm_g = 1.0 + 0.02 * jax.random.normal(ks[1], (DEPTH, D_MODEL), f32)
    conv_w_in = jax.random.normal(ks[2], (n_a, D_MODEL, 4 * CONV_WIDTH), f32) * D_MODEL ** -0.5
    conv_kernel = jax.random.normal(ks[3], (n_a, CONV_K, CONV_WIDTH), f32) * CONV_K ** -0.5
    conv_bias = 0.02 * jax.random.normal(ks[4], (n_a, CONV_WIDTH), f32)
    conv_w_out = jax.random.normal(ks[5], (n_a, CONV_WIDTH, D_MODEL), f32) * CONV_WIDTH ** -0.5
    attn_w_in = jax.random.normal(ks[6], (n_b, D_MODEL, 4 * ATTN_WIDTH), f32) * D_MODEL ** -0.5
    q_norm_g = 1.0 + 0.02 * jax.random.normal(ks[7], (n_b, N_HEADS, HEAD_DIM), f32)
    k_norm_g = 1.0 + 0.02 * jax.random.normal(ks[8], (n_b, N_HEADS, HEAD_DIM), f32)
    attn_w_out = jax.random.normal(ks[9], (n_b, ATTN_WIDTH, D_MODEL), f32) * ATTN_WIDTH ** -0.5
    rel_bias_table = 0.5 * jax.random.normal(ks[10], (REL_BUCKETS, N_HEADS), f32)
    return {"x": x, "norm_g": norm_g, "conv_w_in": conv_w_in, "conv_kernel": conv_kernel,
            "conv_bias": conv_bias, "conv_w_out": conv_w_out, "attn_w_in": attn_w_in,
            "q_norm_g": q_norm_g, "k_norm_g": k_norm_g, "attn_w_out": attn_w_out,
            "rel_bias_table": rel_bias_table}


def reference(x, norm_g, conv_w_in, conv_kernel, conv_bias, conv_w_out, attn_w_in,
              q_norm_g, k_norm_g, attn_w_out, rel_bias_table):
    h = x
    for i in range(DEPTH):
        hn = rmsnorm(h, norm_g[i])
        j = i // N_MIXERS
        if i % N_MIXERS == 0:
            h = h + short_conv_mixer(hn, conv_w_in[j], conv_kernel[j], conv_bias[j], conv_w_out[j])
        else:
            h = h + dilated_attention_mixer(hn, attn_w_in[j], q_norm_g[j], k_norm_g[j],
                                            rel_bias_table, attn_w_out[j])
    return h
```

```python
import numpy as np
import ml_dtypes
from contextlib import ExitStack
import concourse.bass as bass
import concourse.mybir as mybir
from concourse.bass_utils import run_bass_kernel_spmd

F32 = mybir.dt.float32
BF16 = mybir.dt.bfloat16
ALU = mybir.AluOpType
AF = mybir.ActivationFunctionType

S_LEN = 16384
DM = 1024
NCORES = 8
DIRECT_CAST = True
OWN = 4096
HALO = 1024
REG = OWN + 2 * HALO
TA = 473
WA = 475
NTA = 13
XS_ROWS = TA * (NTA - 1) + WA
H1_ROWS = XS_ROWS + 1
EPS = 1e-6
DIL = (1, 4, 16)
VLEN = 384
ENGS = ("pe", "act", "dve", "pool", "sp")


class Op:
    __slots__ = ("eng", "fn", "reads", "writes", "dma", "idx", "waits", "sig", "val")

    def __init__(self, eng, fn, reads, writes, dma):
        self.eng = eng
        self.fn = fn
        self.reads = tuple(reads)
        self.writes = tuple(writes)
        self.dma = dma
        self.waits = []
        self.sig = None
        self.val = 0


class Prog:
    def __init__(self, nc, stack):
        self.nc = nc
        self.stack = stack
        self.ops = []
        self.sems = {}
        self.counts = {}
        self.waited = {e: {} for e in ENGS}

    def add(self, eng, fn, reads=(), writes=(), dma=None):
        writes = list(writes)
        if dma is not None:
            writes.append(("__dmakey", dma))
        self.ops.append(Op(eng, fn, reads, writes, dma))

    def _sem(self, key):
        if key not in self.sems:
            self.sems[key] = self.stack.enter_context(self.nc.semaphore("s%d" % len(self.sems)))
        return self.sems[key]

    def flush(self):
        ops = self.ops
        self.ops = []
        if not ops:
            return
        last_w, readers, deps_of = {}, {}, []
        need = set()
        for i, op in enumerate(ops):
            deps = set()
            for b in op.reads:
                if b in last_w:
                    deps.add(last_w[b])
            for b in op.writes:
                if b in last_w:
                    deps.add(last_w[b])
                deps.update(readers.get(b, ()))
            deps.discard(i)
            d2 = set()
            for d in deps:
                p = ops[d]
                if p.eng == "pe" and op.eng == "pe" and p.dma is None and op.dma is None:
                    continue
                d2.add(d)
            deps_of.append(d2)
            need |= d2
            for b in op.reads:
                readers.setdefault(b, []).append(i)
            for b in op.writes:
                last_w[b] = i
                readers[b] = []
        last_of = {}
        for i, op in enumerate(ops):
            if op.dma is not None:
                need.add(i)
            else:
                last_of[op.eng] = i
        need |= set(last_of.values())
        for i, op in enumerate(ops):
            if i not in need:
                continue
            key = ("dma", op.dma) if op.dma is not None else ("eng", op.eng)
            self.counts[key] = self.counts.get(key, 0) + (16 if op.dma is not None else 1)
            op.sig, op.val = key, self.counts[key]
            self._sem(key)
        for i, op in enumerate(ops):
            w = {}
            for d in deps_of[i]:
                p = ops[d]
                if w.get(p.sig, 0) < p.val:
                    w[p.sig] = p.val
            for k, v in w.items():
                if self.waited[op.eng].get(k, 0) >= v:
                    continue
                self.waited[op.eng][k] = v
                op.waits.append((k, v))
        per = {e: [op for op in ops if op.eng == e] for e in ENGS}
        sems, counts, waited = self.sems, dict(self.counts), self.waited

        def run(name, eng):
            for op in per[name]:
                for k, v in op.waits:
                    eng.wait_ge(sems[k], v)
                ins = op.fn(eng)
                if op.sig is not None:
                    ins.then_inc(sems[op.sig], 16 if op.dma is not None else 1)
            for k, v in counts.items():
                if waited[name].get(k, 0) < v:
                    eng.wait_ge(sems[k], v)
                    waited[name][k] = v

        with self.nc.Block() as block:
            @block.sync
            def _(e):
                run("sp", e)

            @block.tensor
            def _(e):
                run("pe", e)

            @block.scalar
            def _(e):
                run("act", e)

            @block.vector
            def _(e):
                run("dve", e)

            @block.gpsimd
            def _(e):
                run("pool", e)


def t5_bucket(rel):
    nb = 16
    ret = (rel > 0).astype(np.int32) * nb
    n = np.abs(rel)
    max_exact = nb // 2
    large = max_exact + (np.log(np.maximum(n, 1) / max_exact)
                         / np.log(1024 / max_exact) * (nb - max_exact)).astype(np.int32)
    large = np.minimum(large, nb - 1)
    return ret + np.where(n < max_exact, n, large).astype(np.int32)


def host_consts():
    bf = ml_dtypes.bfloat16
    c = {}
    c["ident"] = np.eye(128, dtype=np.float32).astype(bf)
    bo = np.zeros((128, 128), np.float32)
    bo[:64, :64] = 1.0 / 64
    bo[64:, 64:] = 1.0 / 64
    c["bones"] = bo.astype(bf)
    c["jflip"] = np.eye(128, dtype=np.float32)[::-1].copy().astype(bf)
    on = np.zeros((128, 2, 128), np.float32)
    on[:, 0, :64] = 1.0
    on[:, 1, 64:] = 1.0
    c["ones2"] = on.astype(bf)
    oh = np.zeros((32, 3, VLEN), np.float32)
    win = np.zeros((8, VLEN), np.float32)
    m = np.arange(383)
    delta = 191 - m
    for g, d in enumerate(DIL):
        bk = t5_bucket(delta * d)
        oh[bk, g, m] = 1.0
    win[:, :383] = (np.abs(delta) <= 64).astype(np.float32)[None, :]
    c["oh"] = oh
    c["win"] = win
    hm = np.zeros((128, 2), np.float32)
    hm[:64, 0] = 1.0
    hm[64:, 1] = 1.0
    c["hmask"] = hm
    return c


def build(debug=None):
    nc = bass.Bass("TRN2", target_bir_lowering=False)
    _uc = [0]

    def uniq(name):
        _uc[0] += 1
        return "t%d_%s" % (_uc[0], name)

    def din(name, shape, dt=F32):
        return nc.dram_tensor(name, list(shape), dt, kind="ExternalInput")

    xs_h = din("xs", [XS_ROWS, DM])
    ng_h = din("norm_g", [2, DM])
    w0_h = din("conv_w_in", [DM, 8192])
    cw_h = din("cw", [128, 16, 3])
    cb_h = din("cb", [128, 16])
    wo0_h = din("conv_w_out", [2048, DM])
    w1_h = din("attn_w_in", [DM, 6144])
    gq_h = din("gqT", [128, 12])
    gk_h = din("gkT", [128, 12])
    wo1_h = din("attn_w_out", [1536, DM])
    tb_h = din("rel_bias_table", [32, 24])
    fl_h = din("fl", [128, 3, 48])
    ident_h = din("ident", [128, 128], BF16)
    bones_h = din("bones", [128, 128], BF16)
    jflip_h = din("jflip", [128, 128], BF16)
    ones2_h = din("ones2", [128, 2, 128], BF16)
    oh_h = din("oh", [32, 3, VLEN])
    win_h = din("win", [8, VLEN])
    hm_h = din("hmask", [128, 2])
    out_h = nc.dram_tensor("out", [OWN, DM], F32, kind="ExternalOutput")

    H1_h = nc.dram_tensor("H1", [H1_ROWS, DM], F32, kind="ExternalOutput") if debug == "A" else nc.dram_tensor("H1", [H1_ROWS, DM], F32)
    H1a_h = nc.dram_tensor("H1a", [H1_ROWS, DM], F32)
    QS_h = nc.dram_tensor("QS", [12, 2, 128, OWN], BF16)
    KS_h = nc.dram_tensor("KS", [12, 128, REG], BF16)
    ZS_h = nc.dram_tensor("ZS", [12, 128, OWN], BF16)
    VS_h = nc.dram_tensor("VS", [12, REG, 128], BF16)
    YS_h = nc.dram_tensor("YS", [12, 128, OWN], BF16)
    VEC_h = nc.dram_tensor("VEC", [24, VLEN], BF16)
    W0S_h = nc.dram_tensor("W0S", [8, 4, 128, 8, 128], BF16)
    WOS_h = nc.dram_tensor("WOS", [8, 128, DM], BF16)
    W1S_h = nc.dram_tensor("W1S", [48, 128, 8, 128], BF16)
    WO1S_h = nc.dram_tensor("WO1S", [12, 128, DM], BF16)

    xs, ng, w0, cw, cb, wo0, w1 = xs_h.ap(), ng_h.ap(), w0_h.ap(), cw_h.ap(), cb_h.ap(), wo0_h.ap(), w1_h.ap()
    wo1, outp, H1 = wo1_h.ap(), out_h.ap(), H1_h.ap()
    H1a = H1a_h.ap()
    QS, KS, ZS, VS, YS, VEC = QS_h.ap(), KS_h.ap(), ZS_h.ap(), VS_h.ap(), YS_h.ap(), VEC_h.ap()
    W0S, WOS, W1S, WO1S = W0S_h.ap(), WOS_h.ap(), W1S_h.ap(), WO1S_h.ap()

    with ExitStack() as gst:
        P = Prog(nc, gst)

        def gsb(name, shape, dt):
            return gst.enter_context(nc.sbuf_tensor(uniq(name), list(shape), dt))

        ident = gsb("ident", [128, 128], BF16)
        bones = gsb("bones", [128, 128], BF16)
        jflip = gsb("jflip", [128, 128], BF16)
        ones2 = gsb("ones2", [128, 2, 128], BF16)
        EB = gsb("EB", [128, 24 * 2 * 128], BF16)
        FL = gsb("FL", [128, 3, 48], F32)
        NB = gsb("NB", [128, 3, 48], F32)
        gqA = gsb("gqA", [128, 12], F32)
        gqB = gsb("gqB", [128, 12], F32)
        gkS = gsb("gkS", [128, 12], F32)
        mh = gsb("mh", [128, 1], F32)
        cwt = gsb("cwt", [128, 16, 3], F32)
        cbt = gsb("cbt", [128, 16], F32)
        EB4 = EB[:, :].rearrange("p (h i q) -> p h i q", h=24, i=2)

        with ExitStack() as st:
            def sb(name, shape, dt):
                return st.enter_context(nc.sbuf_tensor(uniq(name), list(shape), dt))

            def psum(name, shape, dt):
                return st.enter_context(nc.psum_tensor(uniq(name), list(shape), dt))

            tb = sb("tb", [32, 24], F32)
            oh = sb("oh", [32, 3, VLEN], F32)
            win = sb("win", [8, VLEN], F32)
            hm = sb("hm", [128, 2], F32)
            gtmp = sb("gtmp", [128, 12], F32)
            gtmp2 = sb("gtmp2", [128, 12], F32)
            ev = sb("ev", [8, 3, VLEN], F32)
            evb = sb("evb", [8, 3, VLEN], BF16)
            EBp = sb("EBp", [128, 24 * 2 * 128], BF16)
            EBp4 = EBp[:, :].rearrange("p (h i q) -> p h i q", h=24, i=2)
            pV = psum("pV", [128, 512], F32)
            pE = [psum("pE%d" % i, [128, 512], F32) for i in range(2)]

            for nm, t_, h_ in (("ident", ident, ident_h), ("bones", bones, bones_h), ("jflip", jflip, jflip_h),
                               ("ones2", ones2, ones2_h), ("FL", FL, fl_h), ("tb", tb, tb_h), ("oh", oh, oh_h),
                               ("win", win, win_h), ("hm", hm, hm_h), ("cwt", cwt, cw_h), ("cbt", cbt, cb_h),
                               ("gtmp", gtmp, gq_h), ("gtmp2", gtmp2, gk_h)):
                P.add("sp", lambda e, t_=t_, h_=h_: e.dma_start(out=t_[:], in_=h_.ap()), writes=[nm], dma="const")
            P.add("pool", lambda e: e.memset(mh[:], -0.5), writes=["mh"])
            P.add("dve", lambda e: e.tensor_scalar(out=NB[:], in0=FL[:], scalar1=30000.0, scalar2=-30000.0,
                                                   op0=ALU.mult, op1=ALU.add), reads=["FL"], writes=["NB"])
            P.add("dve", lambda e: e.tensor_scalar(out=gqA[:], in0=gtmp[:], scalar1=0.125, scalar2=hm[:, 0:1],
                                                   op0=ALU.mult, op1=ALU.mult), reads=["gtmp", "hm"], writes=["gqA"])
            P.add("dve", lambda e: e.tensor_scalar(out=gqB[:], in0=gtmp[:], scalar1=0.125, scalar2=hm[:, 1:2],
                                                   op0=ALU.mult, op1=ALU.mult), reads=["gtmp", "hm"], writes=["gqB"])
            P.add("dve", lambda e: e.tensor_copy(out=gkS[:], in_=gtmp2[:]), reads=["gtmp2"], writes=["gkS"])
            for g in range(3):
                P.add("pe", lambda e, g=g: e.matmul(pV[0:8, 0:VLEN], lhsT=tb[:, 8 * g:8 * g + 8], rhs=oh[:, g, :],
                                                    start=True, stop=True), reads=["tb", "oh"], writes=["pV"])
                P.add("act", lambda e, g=g: e.activation(out=ev[:, g, :], in_=pV[0:8, 0:VLEN], func=AF.Exp),
                      reads=["pV"], writes=[("ev", g)])
                P.add("dve", lambda e, g=g: e.tensor_tensor(out=evb[:, g, :], in0=ev[:, g, :], in1=win[:], op=ALU.mult),
                      reads=[("ev", g), "win"], writes=[("evb", g)])
                P.add("sp", lambda e, g=g: e.dma_start(out=VEC[8 * g:8 * g + 8, :], in_=evb[:, g, :]),
                      reads=[("evb", g)], writes=["VEC"], dma="const")
            for i in range(2):
                src = bass.AP(VEC_h, 128 - 128 * i, [[1, 128], [VLEN, 24], [1, 128]])
                P.add("sp", lambda e, i=i, src=src: e.dma_start(out=EBp4[:, :, i, :], in_=src),
                      reads=["VEC"], writes=["EBp"], dma="const")
            for n in range(12):
                pe_ = pE[n % 2]
                P.add("pe", lambda e, n=n, pe_=pe_: e.matmul(pe_[:, :], lhsT=jflip[:], rhs=EBp[:, 512 * n:512 * n + 512],
                                                             start=True, stop=True),
                      reads=["jflip", "EBp"], writes=[("pE", n % 2)])
                P.add("act" if n % 2 else "dve",
                      (lambda e, n=n, pe_=pe_: e.activation(out=EB[:, 512 * n:512 * n + 512], in_=pe_[:, :], func=AF.Copy)) if n % 2
                      else (lambda e, n=n, pe_=pe_: e.tensor_copy(out=EB[:, 512 * n:512 * n + 512], in_=pe_[:, :])),
                      reads=[("pE", n % 2)], writes=[("EB", n)])
            P.flush()

        MS = (128, 128, 128, WA - 384)
        for pa in range(2):
            with ExitStack() as st:
                def sb(name, shape, dt):
                    return st.enter_context(nc.sbuf_tensor(uniq(name), list(shape), dt))

                def psum(name, shape, dt):
                    return st.enter_context(nc.psum_tensor(uniq(name), list(shape), dt))

                NWS = 10
                W0 = sb("W0", [128, 8, 4, 8, 128], BF16)
                WO = sb("WO", [128, 8, DM], BF16)
                wst = [sb("wst%d" % i, [128, 512], F32) for i in range(NWS)]
                gb = sb("gb", [128, DM], F32)
                xa = [sb("xa%d" % i, [128, DM], F32) for i in range(4)]
                xr = [sb("xr%d" % i, [128, DM], F32) for i in range(4)]
                xn = [sb("xn%d" % i, [128, DM], BF16) for i in range(4)]
                ss = [sb("ss%d" % i, [128, 1], F32) for i in range(4)]
                vv = [sb("vv%d" % i, [128, 1], F32) for i in range(4)]
                rs = [sb("rs%d" % i, [128, 1], F32) for i in range(4)]
                xnT = [sb("xnT%d" % i, [128, 8, WA], BF16) for i in range(2)]
                ybuf = sb("ybuf", [128, 8, WA], BF16)
                usb = [sb("usb%d" % i, [128, WA], F32) for i in range(2)]
                cgu = [sb("cgu%d" % i, [128, WA], F32) for i in range(2)]
                szb = [sb("szb%d" % i, [128, WA], BF16) for i in range(2)]
                gat = [sb("gat%d" % i, [128, WA], F32) for i in range(2)]
                tcv = [sb("tcv%d" % i, [128, WA], F32) for i in range(2)]
                NPJ = 4
                pj = [psum("pj%d" % i, [128, 512], F32) for i in range(NPJ)]
                pT = [psum("pT%d" % i, [128, 8, 128], BF16) for i in range(2)]
                po = [psum("po%d" % i, [128, 512], F32) for i in range(2)]

                P.add("sp", lambda e: e.dma_start(out=gb[:], in_=ng[0:1, :].partition_broadcast(128)),
                      writes=["gb"], dma="gb")
                P.add("pool", lambda e: e.memset(ybuf[:], 0.0), writes=["ybuf"])
                wcnt = [0]

                def load_w0(el):
                    e_ = 8 * pa + el
                    if pa == 1:
                        for part in (2, 1, 3, 0):
                            P.add("sp", lambda e, part=part, el=el: e.dma_start(out=W0[:, :, part, el, :], in_=W0S[el, part]),
                                  writes=[("W0", part, el, 0), ("W0", part, el, 1)], dma=("w0d", (4 * el + part) % 4))
                        return
                    if DIRECT_CAST:
                        for part in (2, 1, 3, 0):
                            c0 = part * 2048 + e_ * 128
                            P.add("pool", lambda e, part=part, el=el, c0=c0: e.dma_start(
                                out=W0[:, :, part, el, :], in_=w0[:, c0:c0 + 128].rearrange("(k p) c -> p k c", p=128)),
                                writes=[("W0", part, el, 0), ("W0", part, el, 1)], dma=("w0d", (4 * el + part) % 4))
                        return
                    for part in (2, 1, 3, 0):
                        c0 = part * 2048 + e_ * 128
                        for kh in range(2):
                            sl = wcnt[0] % NWS
                            wcnt[0] += 1
                            P.add("sp", lambda e, sl=sl, c0=c0, kh=kh: e.dma_start(
                                out=wst[sl][:, :].rearrange("p (k c) -> p k c", k=4),
                                in_=w0[kh * 512:(kh + 1) * 512, c0:c0 + 128].rearrange("(k p) c -> p k c", p=128)),
                                writes=[("wst", sl)], dma=("wst", sl))
                            ceng = ("act", "dve", "act", "dve", "pool")[wcnt[0] % 5]
                            if ceng == "act":
                                P.add("act", lambda e, sl=sl, part=part, el=el, kh=kh: e.activation(
                                    out=W0[:, 4 * kh:4 * kh + 4, part, el, :],
                                    in_=wst[sl][:, :].rearrange("p (k c) -> p k c", k=4), func=AF.Copy),
                                    reads=[("wst", sl)], writes=[("W0", part, el, kh)])
                            else:
                                P.add(ceng, lambda e, sl=sl, part=part, el=el, kh=kh: e.tensor_copy(
                                    out=W0[:, 4 * kh:4 * kh + 4, part, el, :],
                                    in_=wst[sl][:, :].rearrange("p (k c) -> p k c", k=4)),
                                    reads=[("wst", sl)], writes=[("W0", part, el, kh)])

                def load_wo(el):
                    e_ = 8 * pa + el
                    if pa == 1:
                        P.add("sp", lambda e, el=el: e.dma_start(out=WO[:, el, :], in_=WOS[el]),
                              writes=[("WO", el, 0), ("WO", el, 1)], dma=("wod", el % 4))
                        return
                    if DIRECT_CAST:
                        P.add("pool", lambda e, el=el, e_=e_: e.dma_start(out=WO[:, el, :], in_=wo0[e_ * 128:(e_ + 1) * 128, :]),
                              writes=[("WO", el, 0), ("WO", el, 1)], dma=("wod", el % 4))
                        return
                    for hf in range(2):
                        sl = wcnt[0] % NWS
                        wcnt[0] += 1
                        P.add("sp", lambda e, sl=sl, e_=e_, hf=hf: e.dma_start(
                            out=wst[sl][:, :], in_=wo0[e_ * 128:(e_ + 1) * 128, hf * 512:(hf + 1) * 512]),
                            writes=[("wst", sl)], dma=("wst", sl))
                        P.add("pool", lambda e, sl=sl, el=el, hf=hf: e.tensor_copy(out=WO[:, el, hf * 512:(hf + 1) * 512], in_=wst[sl][:, :]),
                              reads=[("wst", sl)], writes=[("WO", el, hf)])

                bg = []
                if pa == 0:
                    for el in range(8):
                        for part in (2, 1, 3, 0):
                            c0 = part * 2048 + (8 + el) * 128
                            bg.append((W0S[el, part], w0[:, c0:c0 + 128].rearrange("(k p) c -> p k c", p=128)))
                        bg.append((WOS[el], wo0[(8 + el) * 128:(9 + el) * 128, :]))
                    for n_ in range(48):
                        bg.append((W1S[n_], w1[:, 128 * n_:128 * n_ + 128].rearrange("(k p) c -> p k c", p=128)))
                    for c_ in range(12):
                        bg.append((WO1S[c_], wo1[c_ * 128:(c_ + 1) * 128, :]))
                bgc = [0]

                def bg_cast(n):
                    for _ in range(n):
                        if bg:
                            dst, src = bg.pop(0)
                            P.add("pool", lambda e, dst=dst, src=src: e.dma_start(out=dst, in_=src),
                                  dma=("bg", bgc[0] % 4))
                            bgc[0] += 1

                scnt = [0]
                pjc = [0]
                poc = [0]

                def prep_dma(t):
                    for s in range(4):
                        M = MS[s]
                        r0 = TA * t + 128 * s
                        P.add("sp", lambda e, s=s, r0=r0, M=M: e.dma_start(out=xa[s][0:M, :], in_=xs[r0:r0 + M, :]),
                              writes=[("xa", s)], dma=("xa", s))

                def prep_sub(t, s):
                    M = MS[s]
                    sl = s
                    P.add("act", lambda e, sl=sl, s=s, M=M: e.activation(out=xn[s][0:M, :], in_=xa[sl][0:M, :],
                                                                        func=AF.Square, accum_out=ss[sl][0:M, :]),
                          reads=[("xa", sl)], writes=[("xn", s), ("ss", sl)])
                    P.add("dve", lambda e, sl=sl, M=M: e.tensor_scalar(out=vv[sl][0:M, :], in0=ss[sl][0:M, :],
                                                                      scalar1=1.0 / DM, scalar2=EPS,
                                                                      op0=ALU.mult, op1=ALU.add),
                          reads=[("ss", sl)], writes=[("vv", sl)])
                    P.add("pool", lambda e, sl=sl, M=M: e.tensor_tensor(out=rs[sl][0:M, :], in0=vv[sl][0:M, :],
                                                                       in1=mh[0:M, :], op=ALU.pow),
                          reads=[("vv", sl), "mh"], writes=[("rs", sl)])
                    P.add("dve", lambda e, sl=sl, s=s, M=M: e.scalar_tensor_tensor(
                        out=xn[s][0:M, :], in0=xa[sl][0:M, :], scalar=rs[sl][0:M, :], in1=gb[0:M, :],
                        op0=ALU.mult, op1=ALU.mult),
                        reads=[("xa", sl), ("rs", sl), "gb"], writes=[("xn", s)])

                def prep_load(t):
                    prep_dma(t)
                    for s in range(4):
                        prep_sub(t, s)

                def prep_T(t):
                    xi = t % 2
                    for s in range(4):
                        M = MS[s]
                        pi = s % 2
                        for k in range(8):
                            P.add("pe", lambda e, s=s, k=k, M=M, pi=pi: e.transpose(out=pT[pi][:, k, 0:M],
                                                                                    in_=xn[s][0:M, k * 128:(k + 1) * 128],
                                                                                    identity=ident[0:M, 0:M]),
                                  reads=[("xn", s), "ident"], writes=[("pT", pi)])
                        if s % 2:
                            P.add("act", lambda e, s=s, M=M, pi=pi, xi=xi: e.activation(
                                out=xnT[xi][:, :, 128 * s:128 * s + M], in_=pT[pi][:, :, 0:M], func=AF.Copy),
                                reads=[("pT", pi)], writes=[("xnT", xi)])
                        else:
                            P.add("dve", lambda e, s=s, M=M, pi=pi, xi=xi: e.tensor_copy(
                                out=xnT[xi][:, :, 128 * s:128 * s + M], in_=pT[pi][:, :, 0:M]),
                                reads=[("pT", pi)], writes=[("xnT", xi)])

                def phase1(t, nxt=None):
                    xi = t % 2
                    for el in range(8):
                        if t == 0 and pa == 0 and DIRECT_CAST:
                            if el + 2 < 8:
                                load_w0(el + 2)
                        elif t == 0:
                            load_w0(el)
                        if nxt is not None and el % 2 == 1:
                            prep_sub(nxt, el // 2)
                        e_ = 8 * pa + el
                        st_ = el % 2
                        bank = {}
                        for part in (2, 1, 3, 0):
                            bi = pjc[0] % NPJ
                            pjc[0] += 1
                            bank[part] = bi
                            for k in range(8):
                                P.add("pe", lambda e, bi=bi, k=k, part=part, el=el, xi=xi: e.matmul(
                                    pj[bi][:, 0:WA], lhsT=W0[:, k, part, el, :], rhs=xnT[xi][:, k, :],
                                    start=(k == 0), stop=(k == 7)),
                                    reads=[("W0", part, el, k // 4), ("xnT", xi)], writes=[("pj", bi)])
                        bB, bC, bU, bZ = bank[0], bank[1], bank[2], bank[3]
                        P.add("act", lambda e, st_=st_, bU=bU: e.activation(out=usb[st_][:, :], in_=pj[bU][:, 0:WA], func=AF.Copy),
                              reads=[("pj", bU)], writes=[("usb", st_)])
                        P.add("dve", lambda e, st_=st_, bC=bC: e.tensor_tensor(out=cgu[st_][:, :], in0=pj[bC][:, 0:WA],
                                                                             in1=usb[st_][:, :], op=ALU.mult),
                              reads=[("pj", bC), ("usb", st_)], writes=[("cgu", st_)])
                        P.add("act", lambda e, st_=st_, e_=e_: e.activation(
                            out=tcv[st_][:, 1:WA - 1], in_=cgu[st_][:, 1:WA - 1], func=AF.Identity,
                            scale=cwt[:, e_, 1:2], bias=cbt[:, e_:e_ + 1]),
                            reads=[("cgu", st_), "cwt", "cbt"], writes=[("tcv", st_)])
                        P.add("act", lambda e, st_=st_, bZ=bZ: e.activation(out=szb[st_][:, :], in_=pj[bZ][:, 0:WA], func=AF.Silu),
                              reads=[("pj", bZ)], writes=[("szb", st_)])
                        P.add("dve", lambda e, st_=st_, bB=bB: e.tensor_tensor(out=gat[st_][:, :], in0=pj[bB][:, 0:WA],
                                                                             in1=szb[st_][:, :], op=ALU.mult),
                              reads=[("pj", bB), ("szb", st_)], writes=[("gat", st_)])
                        P.add("dve", lambda e, st_=st_, e_=e_: e.scalar_tensor_tensor(
                            out=tcv[st_][:, 1:WA - 1], in0=cgu[st_][:, 0:WA - 2], scalar=cwt[:, e_, 0:1],
                            in1=tcv[st_][:, 1:WA - 1], op0=ALU.mult, op1=ALU.add),
                            reads=[("cgu", st_), ("tcv", st_), "cwt"], writes=[("tcv", st_)])
                        P.add("dve", lambda e, st_=st_, e_=e_: e.scalar_tensor_tensor(
                            out=tcv[st_][:, 1:WA - 1], in0=cgu[st_][:, 2:WA], scalar=cwt[:, e_, 2:3],
                            in1=tcv[st_][:, 1:WA - 1], op0=ALU.mult, op1=ALU.add),
                            reads=[("cgu", st_), ("tcv", st_), "cwt"], writes=[("tcv", st_)])
                        P.add("pool", lambda e, st_=st_, el=el: e.tensor_tensor(
                            out=ybuf[:, el, 1:WA - 1], in0=tcv[st_][:, 1:WA - 1], in1=gat[st_][:, 1:WA - 1], op=ALU.mult),
                            reads=[("tcv", st_), ("gat", st_)], writes=[("ybuf", el)])
                        if t == 0:
                            load_wo(el)
                        elif t >= 2:
                            bg_cast(2)

                def xr_dma(t):
                    for s in range(4):
                        M = MS[s]
                        plo = 1 if s == 0 else 0
                        phi = min(M, TA + 1 - 128 * s)
                        if t == 0 and s == 0:
                            plo = 0
                        if t == NTA - 1 and s == 3:
                            phi = M
                        jlo = TA * t + 128 * s + plo - 1
                        n = phi - plo
                        if pa == 0:
                            r0 = TA * t + 128 * s
                            P.add("sp", lambda e, s=s, r0=r0, M=M: e.dma_start(out=xr[s][0:M, :], in_=xs[r0:r0 + M, :]),
                                  writes=[("xr", s)], dma=("xr", s))
                        else:
                            r0 = TA * t + 128 * s
                            P.add("sp", lambda e, s=s, r0=r0, M=M: e.dma_start(out=xr[s][0:M, :], in_=H1a[r0:r0 + M, :]),
                                  writes=[("xr", s)], dma=("xr", s))

                def phase2(t):
                    for s in range(4):
                        M = MS[s]
                        sl = s
                        plo = 1 if s == 0 else 0
                        phi = min(M, TA + 1 - 128 * s)
                        if t == 0 and s == 0:
                            plo = 0
                        if t == NTA - 1 and s == 3:
                            phi = M
                        jlo = TA * t + 128 * s + plo - 1
                        n = phi - plo
                        for half in range(2):
                            bi = poc[0] % 2
                            poc[0] += 1
                            for el in range(8):
                                P.add("pe", lambda e, bi=bi, el=el, s=s, M=M, half=half: e.matmul(
                                    po[bi][0:M, :], lhsT=ybuf[:, el, 128 * s:128 * s + M],
                                    rhs=WO[:, el, half * 512:(half + 1) * 512], start=(el == 0), stop=(el == 7)),
                                    reads=[("ybuf", el), ("WO", el, half)], writes=[("po", bi)])
                            P.add("dve", lambda e, bi=bi, sl=sl, M=M, half=half: e.tensor_tensor(
                                out=xr[sl][0:M, half * 512:(half + 1) * 512], in0=po[bi][0:M, :],
                                in1=xr[sl][0:M, half * 512:(half + 1) * 512], op=ALU.add),
                                reads=[("po", bi), ("xr", sl)], writes=[("xr", sl)])
                        r0 = TA * t + 128 * s
                        n1 = (n // 16) * 16
                        for pi_, (a_, b_) in enumerate(((plo, plo + n1), (plo + n1, plo + n))):
                            if b_ > a_:
                                P.add("pool", lambda e, sl=sl, a_=a_, b_=b_, r0=r0: e.dma_start(
                                    out=(H1a if pa == 0 else H1)[r0 + a_:r0 + b_, :], in_=xr[sl][a_:b_, :]),
                                    reads=[("xr", sl)], writes=[("H1", t, s, pi_)], dma=("h1w", sl, pi_))

                if pa == 0 and DIRECT_CAST:
                    load_w0(0)
                    load_w0(1)
                prep_load(0)
                prep_T(0)
                for t in range(NTA):
                    xr_dma(t)
                    if t + 1 < NTA:
                        prep_dma(t + 1)
                    phase1(t, t + 1 if t + 1 < NTA else None)
                    if t + 1 < NTA:
                        prep_T(t + 1)
                    phase2(t)
                P.flush()

        if debug == "A":
            return nc

        with ExitStack() as st:
            def sb(name, shape, dt):
                return st.enter_context(nc.sbuf_tensor(uniq(name), list(shape), dt))

            def psum(name, shape, dt):
                return st.enter_context(nc.psum_tensor(uniq(name), list(shape), dt))

            W1 = sb("W1", [128, 8, 6144], BF16)
            wst = [sb("wstb%d" % i, [128, 1024], F32) for i in range(6)]
            gb = sb("gb1", [128, DM], F32)
            xa = [sb("xab%d" % i, [128, DM], F32) for i in range(4)]
            xn = [sb("xnb%d" % i, [128, DM], BF16) for i in range(4)]
            ss = [sb("ssb%d" % i, [128, 1], F32) for i in range(4)]
            vv = [sb("vvb%d" % i, [128, 1], F32) for i in range(4)]
            rs = [sb("rsb%d" % i, [128, 1], F32) for i in range(4)]
            hnTs = [sb("hnT%d" % i, [128, 8, 512], BF16) for i in range(2)]
            cur = {"h": 0}
            sq = [sb("sq%d" % i, [128, 512], BF16) for i in range(2)]
            lnv = [sb("lnv%d" % i, [128, 512], F32) for i in range(2)]
            rq = [sb("rq%d" % i, [128, 512], F32) for i in range(2)]
            qa = [sb("qa%d" % i, [128, 512], BF16) for i in range(2)]
            qb = [sb("qb%d" % i, [128, 512], BF16) for i in range(2)]
            kk = [sb("kk%d" % i, [128, 512], BF16) for i in range(2)]
            e1 = [sb("e1%d" % i, [128, 512], F32) for i in range(2)]
            zo = [sb("zo%d" % i, [128, 512], BF16) for i in range(2)]
            vo = [sb("vo%d" % i, [128, 512], BF16) for i in range(2)]
            pj = [psum("pjb%d" % i, [128, 512], F32) for i in range(5)]
            pn = [psum("pnb%d" % i, [128, 512], F32) for i in range(2)]
            pT = psum("pTb", [128, 8, 128], BF16)

            P.add("sp", lambda e: e.dma_start(out=gb[:], in_=ng[1:2, :].partition_broadcast(128)),
                  writes=["gb"], dma="gb")
            col_order = []
            for g in (2, 0, 1):
                col_order += [g * 1536 + 512 + hp * 128 for hp in range(4)]
                col_order += [g * 1536 + 1024 + i * 128 for i in range(4)]
            for g in range(3):
                for hp in range(4):
                    col_order += [g * 1536 + hp * 128, 4608 + g * 512 + hp * 128]
            assert len(set(col_order)) == 48
            wloaded = set()
            wn = [0]

            def ensure_w(c0):
                if c0 in wloaded:
                    return
                wloaded.add(c0)
                n = wn[0]
                wn[0] += 1
                P.add("sp", lambda e, c0=c0: e.dma_start(out=W1[:, :, c0:c0 + 128], in_=W1S[c0 // 128]),
                      writes=[("W1", c0 // 128)], dma=("w1d", n % 4))

            scnt = 0
            pjc = [0]
            pnc = [0]
            cnt = {"q": 0, "k": 0, "z": 0, "v": 0}

            NPJB = 5

            def proj(c0):
                ensure_w(c0)
                bi = pjc[0] % NPJB
                pjc[0] += 1
                for k in range(8):
                    hi = cur["h"]
                    P.add("pe", lambda e, bi=bi, k=k, c0=c0, hi=hi: e.matmul(pj[bi][:, :], lhsT=W1[:, k, c0:c0 + 128],
                                                                             rhs=hnTs[hi][:, k, :], start=(k == 0), stop=(k == 7)),
                          reads=[("W1", c0 // 128), ("hnT", hi)], writes=[("pj", bi)])
                return bi

            def item_qk(kind, c, c0, pos):
                st = {}

                def s1():
                    bi = proj(c0)
                    ni = pnc[0] % 2
                    pnc[0] += 1
                    st["bi"], st["ni"] = bi, ni
                    P.add("act", lambda e: e.activation(out=sq[ni][:, :], in_=pj[bi][:, :], func=AF.Square),
                          reads=[("pj", bi)], writes=[("sq", ni)])

                def s2():
                    bi, ni = st["bi"], st["ni"]
                    P.add("pe", lambda e: e.matmul(pn[ni][:, :], lhsT=bones[:], rhs=sq[ni][:, :], start=True, stop=True),
                          reads=["bones", ("sq", ni)], writes=[("pn", ni)])
                    P.add("act", lambda e: e.activation(out=lnv[ni][:, :], in_=pn[ni][:, :], func=AF.Ln, bias=EPS),
                          reads=[("pn", ni)], writes=[("lnv", ni)])
                    P.add("act", lambda e: e.activation(out=rq[ni][:, :], in_=lnv[ni][:, :], func=AF.Exp, scale=-0.5),
                          reads=[("lnv", ni)], writes=[("rq", ni)])
                    if kind == "q":
                        qi = cnt["q"] % 2
                        cnt["q"] += 1
                        P.add("dve", lambda e: e.scalar_tensor_tensor(
                            out=qa[qi][:, :], in0=pj[bi][:, :], scalar=gqA[:, c:c + 1], in1=rq[ni][:, :],
                            op0=ALU.mult, op1=ALU.mult),
                            reads=[("pj", bi), ("rq", ni), "gqA"], writes=[("qa", qi)])
                        P.add("dve", lambda e: e.scalar_tensor_tensor(
                            out=qb[qi][:, :], in0=pj[bi][:, :], scalar=gqB[:, c:c + 1], in1=rq[ni][:, :],
                            op0=ALU.mult, op1=ALU.mult),
                            reads=[("pj", bi), ("rq", ni), "gqB"], writes=[("qb", qi)])
                        P.add("pool", lambda e: e.dma_start(out=QS[c, 0, :, pos:pos + 512], in_=qa[qi][:, :]),
                              reads=[("qa", qi)], writes=[("QSa", c, pos)], dma=("qa", qi))
                        P.add("pool", lambda e: e.dma_start(out=QS[c, 1, :, pos:pos + 512], in_=qb[qi][:, :]),
                              reads=[("qb", qi)], writes=[("QSb", c, pos)], dma=("qb", qi))
                    else:
                        ki = cnt["k"] % 2
                        cnt["k"] += 1
                        P.add("dve", lambda e: e.scalar_tensor_tensor(
                            out=kk[ki][:, :], in0=pj[bi][:, :], scalar=gkS[:, c:c + 1], in1=rq[ni][:, :],
                            op0=ALU.mult, op1=ALU.mult),
                            reads=[("pj", bi), ("rq", ni), "gkS"], writes=[("kk", ki)])
                        P.add("pool", lambda e: e.dma_start(out=KS[c, :, pos:pos + 512], in_=kk[ki][:, :]),
                              reads=[("kk", ki)], writes=[("KS", c, pos)], dma=("kk", ki))
                return s1, s2

            def item_z(c, c0, pos):
                st = {}

                def s1():
                    bi = proj(c0)
                    zi = cnt["z"] % 2
                    cnt["z"] += 1
                    st["bi"], st["zi"] = bi, zi
                    P.add("act", lambda e: e.activation(out=e1[zi][:, :], in_=pj[bi][:, :], func=AF.Exp, scale=-1.0),
                          reads=[("pj", bi)], writes=[("e1", zi)])

                def s2():
                    bi, zi = st["bi"], st["zi"]
                    P.add("act", lambda e: e.activation(out=e1[zi][:, :], in_=e1[zi][:, :], func=AF.Ln, bias=1.0),
                          reads=[("e1", zi)], writes=[("e1", zi)])
                    P.add("act", lambda e: e.activation(out=e1[zi][:, :], in_=e1[zi][:, :], func=AF.Exp, scale=-1.0),
                          reads=[("e1", zi)], writes=[("e1", zi)])
                    P.add("dve", lambda e: e.tensor_tensor(out=zo[zi][:, :], in0=pj[bi][:, :], in1=e1[zi][:, :], op=ALU.mult),
                          reads=[("pj", bi), ("e1", zi)], writes=[("zo", zi)])
                    P.add("pool", lambda e: e.dma_start(out=ZS[c, :, pos:pos + 512], in_=zo[zi][:, :]),
                          reads=[("zo", zi)], writes=[("ZS", c, pos)], dma=("zo", zi))
                return s1, s2

            def item_v(g, s_, r0):
                st = {}

                def s1():
                    bi = pjc[0] % NPJB
                    pjc[0] += 1
                    st["bi"] = bi
                    vc0 = g * 1536 + 1024
                    for i_ in range(4):
                        ensure_w(vc0 + 128 * i_)
                    hi = cur["h"]
                    for k in range(8):
                        P.add("pe", lambda e, k=k, hi=hi: e.matmul(
                            pj[bi][:, :], lhsT=hnTs[hi][:, k, 128 * s_:128 * s_ + 128], rhs=W1[:, k, vc0:vc0 + 512],
                            start=(k == 0), stop=(k == 7)),
                            reads=[("hnT", hi)] + [("W1", vc0 // 128 + i) for i in range(4)], writes=[("pj", bi)])

                def s2():
                    bi = st["bi"]
                    vi = cnt["v"] % 2
                    cnt["v"] += 1
                    if vi:
                        P.add("act", lambda e: e.activation(out=vo[vi][:, :], in_=pj[bi][:, :], func=AF.Copy),
                              reads=[("pj", bi)], writes=[("vo", vi)])
                    else:
                        P.add("dve", lambda e: e.tensor_copy(out=vo[vi][:, :], in_=pj[bi][:, :]),
                              reads=[("pj", bi)], writes=[("vo", vi)])
                    P.add("pool", lambda e: e.dma_start(
                        out=VS[4 * g:4 * g + 4, r0:r0 + 128, :].rearrange("h t c -> t h c"),
                        in_=vo[vi][:, :].rearrange("t (h c) -> t h c", h=4)),
                        reads=[("vo", vi)], writes=[("VS", g, r0)], dma=("vo", vi))
                return s1, s2

            def prep_dma_B(tt):
                j0 = 512 * tt
                for s in range(4):
                    r0 = j0 + 128 * s
                    P.add("sp", lambda e, s=s, r0=r0: e.dma_start(out=xa[s][:, :], in_=H1[r0 + 1:r0 + 129, :]),
                          writes=[("xa", s)], dma=("xa", s))

            def prep_sub_B(tt, s):
                sl = s
                P.add("act", lambda e, sl=sl, s=s: e.activation(out=xn[s][:, :], in_=xa[sl][:, :], func=AF.Square,
                                                               accum_out=ss[sl][:, :]),
                      reads=[("xa", sl)], writes=[("xn", s), ("ss", sl)])
                P.add("dve", lambda e, sl=sl: e.tensor_scalar(out=vv[sl][:, :], in0=ss[sl][:, :], scalar1=1.0 / DM,
                                                             scalar2=EPS, op0=ALU.mult, op1=ALU.add),
                      reads=[("ss", sl)], writes=[("vv", sl)])
                P.add("pool", lambda e, sl=sl: e.tensor_tensor(out=rs[sl][:, :], in0=vv[sl][:, :], in1=mh[:, :], op=ALU.pow),
                      reads=[("vv", sl), "mh"], writes=[("rs", sl)])
                P.add("dve", lambda e, sl=sl, s=s: e.scalar_tensor_tensor(
                    out=xn[s][:, :], in0=xa[sl][:, :], scalar=rs[sl][:, :], in1=gb[:, :], op0=ALU.mult, op1=ALU.mult),
                    reads=[("xa", sl), ("rs", sl), "gb"], writes=[("xn", s)])

            def prep_load_B(tt):
                prep_dma_B(tt)
                for s in range(4):
                    prep_sub_B(tt, s)

            def prep_T_B(tt):
                hi = tt % 2
                for s in range(4):
                    for k in range(8):
                        P.add("pe", lambda e, s=s, k=k: e.transpose(out=pT[:, k, :], in_=xn[s][:, k * 128:(k + 1) * 128],
                                                                    identity=ident[:, :]),
                              reads=[("xn", s), "ident"], writes=["pT"])
                    if s % 2:
                        P.add("act", lambda e, s=s, hi=hi: e.activation(out=hnTs[hi][:, :, 128 * s:128 * s + 128], in_=pT[:, :, :], func=AF.Copy),
                              reads=["pT"], writes=[("hnT", hi)])
                    else:
                        P.add("dve", lambda e, s=s, hi=hi: e.tensor_copy(out=hnTs[hi][:, :, 128 * s:128 * s + 128], in_=pT[:, :, :]),
                              reads=["pT"], writes=[("hnT", hi)])

            pending = []
            pf = []
            for g in range(3):
                for hp in range(4):
                    pf += [g * 1536 + hp * 128, 4608 + g * 512 + hp * 128]
            prep_load_B(0)
            prep_T_B(0)
            for tt in range(12):
                j0 = 512 * tt
                own = 2 <= tt < 10
                kv01 = 1 <= tt <= 10
                cur["h"] = tt % 2
                if tt + 1 < 12:
                    prep_dma_B(tt + 1)
                items = []
                for g in range(3):
                    for hp in range(4):
                        c = g * 4 + hp
                        if own:
                            items.append(item_qk("q", c, g * 1536 + hp * 128, (tt - 2) * 512))
                        if g == 2 or kv01:
                            items.append(item_qk("k", c, g * 1536 + 512 + hp * 128, j0))
                        if own:
                            items.append(item_z(c, 4608 + g * 512 + hp * 128, (tt - 2) * 512))
                for g in range(3):
                    if g == 2 or kv01:
                        for s in range(4):
                            items.append(item_v(g, s, j0 + 128 * s))
                for n, (s1, s2) in enumerate(items):
                    s1()
                    if tt >= 1 and pf:
                        ensure_w(pf.pop(0))
                    for p2 in pending:
                        p2()
                    pending = [s2]
                    if tt + 1 < 12:
                        L_ = len(items)
                        for s_ in range(4):
                            if n == (s_ + 1) * L_ // 10:
                                prep_sub_B(tt + 1, s_)
                        if n == L_ // 2:
                            prep_T_B(tt + 1)
            for s2 in pending:
                s2()
            P.flush()

        if debug == "B":
            return nc

        with ExitStack() as st:
            def sb(name, shape, dt):
                return st.enter_context(nc.sbuf_tensor(uniq(name), list(shape), dt))

            def psum(name, shape, dt):
                return st.enter_context(nc.psum_tensor(uniq(name), list(shape), dt))

            QA = [sb("QA%d" % i, [128, OWN], BF16) for i in range(2)]
            QB = [sb("QB%d" % i, [128, OWN], BF16) for i in range(2)]
            KT = [sb("KT%d" % i, [128, REG], BF16) for i in range(2)]
            ZT = [sb("ZT%d" % i, [128, OWN], BF16) for i in range(2)]
            Vb = sb("Vb", [128, 48, 2, 128], BF16)
            Dn = sb("Dn", [128, OWN], F32)
            U = [sb("U%d" % i, [128, OWN], F32) for i in range(3)]
            yb = [sb("yb%d" % i, [128, OWN], BF16) for i in range(2)]
            pex = [sb("pex%d" % i, [128, 2, 2, 128], BF16) for i in range(4)]
            pm = [sb("pm%d" % i, [128, 2, 2, 128], BF16) for i in range(4)]
            pS = [psum("pS%d" % i, [128, 2, 2, 128], F32) for i in range(4)]
            pO = [psum("pO%d" % i, [128, 128], F32) for i in range(2)]
            pD = [psum("pD%d" % i, [128, 128], F32) for i in range(2)]

            P.add("pool", lambda e: e.memset(Vb[:], 0.0), writes=["Vbz"])
            cc = 0
            bc = 0
            yc = 0
            for hp in range(4):
                for g in range(3):
                    c = g * 4 + hp
                    d = DIL[g]
                    Lk = OWN + 128 * d
                    reg0 = HALO - 64 * d
                    nti = 32 // d + 1
                    nb = 32 // d
                    st_ = cc % 2
                    cc += 1
                    P.add("sp", lambda e, st_=st_, c=c: e.dma_start(out=QA[st_][:, :], in_=QS[c, 0, :, :]),
                          writes=[("QA", st_)], dma=("QA", st_))
                    P.add("sp", lambda e, st_=st_, c=c: e.dma_start(out=QB[st_][:, :], in_=QS[c, 1, :, :]),
                          writes=[("QB", st_)], dma=("QB", st_))
                    P.add("sp", lambda e, st_=st_, c=c, reg0=reg0, Lk=Lk: e.dma_start(out=KT[st_][:, 0:Lk], in_=KS[c, :, reg0:reg0 + Lk]),
                          writes=[("KT", st_)], dma=("KT", st_))
                    P.add("sp", lambda e, st_=st_, c=c: e.dma_start(out=ZT[st_][:, :], in_=ZS[c, :, :]),
                          writes=[("ZT", st_)], dma=("ZT", st_))
                    nv = 0
                    for r in range(d):
                        for i0 in range(0, nti, 8):
                            ni_ = min(8, nti - i0)
                            for hh in range(2):
                                off = (c * REG + reg0 + r + d * 128 * i0) * 128 + 64 * hh
                                src = bass.AP(VS_h, off, [[d * 128, 128], [128 * d * 128, ni_], [1, 64]])
                                n0 = r * nti + i0
                                P.add("sp", lambda e, src=src, n0=n0, ni_=ni_, hh=hh: e.dma_start(
                                    out=Vb[:, n0:n0 + ni_, hh, 64 * hh:64 * hh + 64], in_=src),
                                    reads=["Vbz"], writes=[("Vb", hh, n_) for n_ in range(n0, n0 + ni_)], dma=("Vb", nv % 12))
                                nv += 1
                    def stageS(r, b, bi, st_=st_, d=d, nti=nti, g=g, hp=hp):
                        q0 = r + 128 * b * d
                        qsl = slice(q0, q0 + 127 * d + 1, d) if d > 1 else slice(q0, q0 + 128)
                        for hh in range(2):
                            Qt = QA if hh == 0 else QB
                            for i in range(2):
                                k0 = r + 128 * (b + i) * d
                                ksl = slice(k0, k0 + 127 * d + 1, d) if d > 1 else slice(k0, k0 + 128)
                                P.add("pe", lambda e, hh=hh, i=i, ksl=ksl, Qt=Qt: e.matmul(
                                    pS[bi][:, hh, i, :], lhsT=KT[st_][:, ksl], rhs=Qt[st_][:, qsl], start=True, stop=True),
                                    reads=[("KT", st_), ("QA", st_), ("QB", st_)], writes=[("pS", bi)])
                        for i in range(2):
                            tn = r * nti + b + i
                            P.add("act", lambda e, i=i, tn=tn: e.activation(out=pex[bi][:, :, i, :], in_=pS[bi][:, :, i, :], func=AF.Exp,
                                                                            bias=NB[:, g, tn:tn + 1]),
                                  reads=[("pS", bi), "NB"], writes=[("pex", bi, i)])
                        meng = "pool" if bi == 3 else "dve"
                        P.add(meng, lambda e: e.tensor_tensor(
                            out=pm[bi][:], in0=pex[bi][:], in1=EB4[:, 8 * g + 2 * hp:8 * g + 2 * hp + 2, :, :], op=ALU.mult),
                            reads=[("pex", bi, 0), ("pex", bi, 1), "EB"], writes=[("pm", bi, 0), ("pm", bi, 1)])

                    def stageO(r, b, bi, oi, st_=st_, d=d, nti=nti, g=g):
                        q0 = r + 128 * b * d
                        qsl = slice(q0, q0 + 127 * d + 1, d) if d > 1 else slice(q0, q0 + 128)
                        n_ = 0
                        for hh in range(2):
                            for i in range(2):
                                tn = r * nti + b + i
                                P.add("pe", lambda e, hh=hh, i=i, tn=tn, n_=n_: e.matmul(
                                    pO[oi][:, :], lhsT=Vb[:, tn, hh, :], rhs=pm[bi][:, hh, i, :],
                                    start=(n_ == 0), stop=(n_ == 3)),
                                    reads=[("Vb", hh, tn), ("pm", bi, i)], writes=[("pO", oi)])
                                n_ += 1
                        n_ = 0
                        for hh in range(2):
                            for i in range(2):
                                P.add("pe", lambda e, hh=hh, i=i, n_=n_: e.matmul(
                                    pD[oi][:, :], lhsT=ones2[:, hh, :], rhs=pm[bi][:, hh, i, :],
                                    start=(n_ == 0), stop=(n_ == 3)),
                                    reads=["ones2", ("pm", bi, i)], writes=[("pD", oi)])
                                n_ += 1
                        P.add("dve", lambda e: e.tensor_tensor(
                            out=U[g][:, qsl], in0=pO[oi][:, :], in1=ZT[st_][:, qsl], op=ALU.mult),
                            reads=[("pO", oi), ("ZT", st_)], writes=[("U", g)])
                        if g == 0:
                            P.add("act", lambda e: e.activation(out=Dn[:, qsl], in_=pD[oi][:, :], func=AF.Copy),
                                  reads=[("pD", oi)], writes=["Dn"])
                        else:
                            P.add("dve", lambda e: e.tensor_tensor(
                                out=Dn[:, qsl], in0=pD[oi][:, :], in1=Dn[:, qsl], op=ALU.add),
                                reads=[("pD", oi), "Dn"], writes=["Dn"])

                    blocks = [(r, b) for r in range(d) for b in range(nb)]
                    inflight = []
                    for blk in blocks + [None, None, None]:
                        if blk is not None:
                            bi = bc % 4
                            oi = bc % 2
                            bc += 1
                            stageS(blk[0], blk[1], bi)
                            inflight.append((blk[0], blk[1], bi, oi))
                        if len(inflight) > 3 or (blk is None and inflight):
                            stageO(*inflight.pop(0))
                    if g == 2:
                        P.add("act", lambda e: e.activation(out=Dn[:, :], in_=Dn[:, :], func=AF.Ln), reads=["Dn"], writes=["Dn"])
                        P.add("act", lambda e: e.activation(out=Dn[:, :], in_=Dn[:, :], func=AF.Exp, scale=-1.0),
                              reads=["Dn"], writes=["Dn"])
                        for g2 in range(3):
                            yi = yc % 2
                            yc += 1
                            c2 = g2 * 4 + hp
                            P.add("dve" if g2 != 1 else "pool", lambda e, g2=g2, yi=yi: e.tensor_tensor(
                                out=yb[yi][:, :], in0=U[g2][:, :], in1=Dn[:, :], op=ALU.mult),
                                reads=[("U", g2), "Dn"], writes=[("yb", yi)])
                            P.add("pool", lambda e, yi=yi, c2=c2: e.dma_start(out=YS[c2, :, :], in_=yb[yi][:, :]),
                                  reads=[("yb", yi)], writes=[("YS", c2)], dma=("yb", yi))
            P.flush()

        if debug == "C":
            return nc

        with ExitStack() as st:
            def sb(name, shape, dt):
                return st.enter_context(nc.sbuf_tensor(uniq(name), list(shape), dt))

            def psum(name, shape, dt):
                return st.enter_context(nc.psum_tensor(uniq(name), list(shape), dt))

            WO1 = sb("WO1", [128, 12, DM], BF16)
            wst = [sb("wstd%d" % i, [128, DM], F32) for i in range(4)]
            yT = [sb("yT%d" % i, [128, 12, 512], BF16) for i in range(2)]
            h1s = [sb("h1s%d" % i, [128, DM], F32) for i in range(3)]
            po = [psum("pod%d" % i, [128, 512], F32) for i in range(3)]
            for c in range(12):
                P.add("sp", lambda e, c=c: e.dma_start(out=WO1[:, c, :], in_=WO1S[c]),
                      writes=[("WO1", c)], dma=("wo1d", c % 4))
            hc = 0
            pc = 0
            for tt in range(8):
                yi = tt % 2
                P.add("sp", lambda e, yi=yi, tt=tt: e.dma_start(
                    out=yT[yi][:, :, :], in_=YS[:, :, 512 * tt:512 * tt + 512].rearrange("c p t -> p c t")),
                    writes=[("yT", yi)], dma=("yT", yi))
                for s in range(4):
                    hi = hc % 3
                    hc += 1
                    r0 = HALO + 512 * tt + 128 * s
                    P.add("sp", lambda e, hi=hi, r0=r0: e.dma_start(out=h1s[hi][:, :], in_=H1[r0 + 1:r0 + 129, :]),
                          writes=[("h1s", hi)], dma=("h1s", hi))
                    for half in range(2):
                        bi = pc % 3
                        pc += 1
                        for c in range(12):
                            P.add("pe", lambda e, bi=bi, c=c, yi=yi, s=s, half=half: e.matmul(
                                po[bi][:, :], lhsT=yT[yi][:, c, 128 * s:128 * s + 128],
                                rhs=WO1[:, c, half * 512:(half + 1) * 512], start=(c == 0), stop=(c == 11)),
                                reads=[("yT", yi), ("WO1", c)], writes=[("po", bi)])
                        P.add("dve", lambda e, bi=bi, hi=hi, half=half: e.tensor_tensor(
                            out=h1s[hi][:, half * 512:(half + 1) * 512], in0=po[bi][:, :],
                            in1=h1s[hi][:, half * 512:(half + 1) * 512], op=ALU.add),
                            reads=[("po", bi), ("h1s", hi)], writes=[("h1s", hi)])
                    o0 = 512 * tt + 128 * s
                    P.add("pool", lambda e, hi=hi, o0=o0: e.dma_start(out=outp[o0:o0 + 128, :], in_=h1s[hi][:, :]),
                          reads=[("h1s", hi)], dma=("outw", hi))
            P.flush()
    return nc


def make_in_maps(x, norm_g, conv_w_in, conv_kernel, conv_bias, conv_w_out, attn_w_in,
                 q_norm_g, k_norm_g, attn_w_out, rel_bias_table):
    consts = host_consts()
    f32 = np.float32
    shared = {
        "norm_g": np.ascontiguousarray(norm_g, f32),
        "conv_w_in": np.ascontiguousarray(conv_w_in[0], f32),
        "cw": np.ascontiguousarray(conv_kernel[0].reshape(3, 16, 128).transpose(2, 1, 0), f32),
        "cb": np.ascontiguousarray(conv_bias[0].reshape(16, 128).T, f32),
        "conv_w_out": np.ascontiguousarray(conv_w_out[0], f32),
        "attn_w_in": np.ascontiguousarray(attn_w_in[0], f32),
        "gqT": np.ascontiguousarray(q_norm_g[0].reshape(12, 128).T, f32),
        "gkT": np.ascontiguousarray(k_norm_g[0].reshape(12, 128).T, f32),
        "attn_w_out": np.ascontiguousarray(attn_w_out[0], f32),
        "rel_bias_table": np.ascontiguousarray(rel_bias_table, f32),
    }
    shared.update(consts)
    in_maps = []
    for c in range(NCORES):
        b, q = divmod(c, 4)
        own0 = q * OWN
        xs = np.zeros((XS_ROWS, DM), f32)
        p0 = own0 - HALO - 1
        lo, hi = max(0, p0), min(S_LEN, p0 + XS_ROWS)
        xs[lo - p0:hi - p0] = x[b, lo:hi]
        fl = np.zeros((128, 3, 48), f32)
        for g, d in enumerate(DIL):
            nti = 32 // d + 1
            reg0 = HALO - 64 * d
            t = np.arange(128)[:, None, None]
            r = np.arange(d)[None, :, None]
            i = np.arange(nti)[None, None, :]
            j = reg0 + r + d * (128 * i + t)
            p = own0 - HALO + j
            fl[:, g, :d * nti] = ((p >= 0) & (p < S_LEN)).reshape(128, d * nti).astype(f32)
        m = dict(shared)
        m["xs"] = xs
        m["fl"] = fl
        in_maps.append(m)
    return in_maps


def kernel(x, norm_g, conv_w_in, conv_kernel, conv_bias, conv_w_out, attn_w_in,
           q_norm_g, k_norm_g, attn_w_out, rel_bias_table):
    args = [np.asarray(a) for a in (x, norm_g, conv_w_in, conv_kernel, conv_bias, conv_w_out, attn_w_in,
                                    q_norm_g, k_norm_g, attn_w_out, rel_bias_table)]
    in_maps = make_in_maps(*args)
    nc = build()
    res = run_bass_kernel_spmd(nc, in_maps, core_ids=list(range(NCORES)))
    out = np.empty((2, S_LEN, DM), np.float32)
    for c in range(NCORES):
        b, q = divmod(c, 4)
        out[b, q * OWN:(q + 1) * OWN] = res.results[c]["out"]
    return out
```

```python
import numpy as np
import ml_dtypes
from contextlib import ExitStack
import concourse.bass as bass
import concourse.mybir as mybir
from concourse.bass_utils import run_bass_kernel_spmd

F32 = mybir.dt.float32
BF16 = mybir.dt.bfloat16
ALU = mybir.AluOpType
AF = mybir.ActivationFunctionType

S_LEN = 16384
DM = 1024
NCORES = 8
DIRECT_CAST = True
OWN = 4096
HALO = 1024
REG = OWN + 2 * HALO
TA = 473
WA = 475
NTA = 13
XS_ROWS = TA * (NTA - 1) + WA
H1_ROWS = XS_ROWS + 1
EPS = 1e-6
DIL = (1, 4, 16)
VLEN = 384
ENGS = ("pe", "act", "dve", "pool", "sp")


class Op:
    __slots__ = ("eng", "fn", "reads", "writes", "dma", "idx", "waits", "sig", "val")

    def __init__(self, eng, fn, reads, writes, dma):
        self.eng = eng
        self.fn = fn
        self.reads = tuple(reads)
        self.writes = tuple(writes)
        self.dma = dma
        self.waits = []
        self.sig = None
        self.val = 0


class Prog:
    def __init__(self, nc, stack):
        self.nc = nc
        self.stack = stack
        self.ops = []
        self.sems = {}
        self.counts = {}
        self.waited = {e: {} for e in ENGS}

    def add(self, eng, fn, reads=(), writes=(), dma=None):
        writes = list(writes)
        if dma is not None:
            writes.append(("__dmakey", dma))
        self.ops.append(Op(eng, fn, reads, writes, dma))

    def _sem(self, key):
        if key not in self.sems:
            self.sems[key] = self.stack.enter_context(self.nc.semaphore("s%d" % len(self.sems)))
        return self.sems[key]

    def flush(self):
        ops = self.ops
        self.ops = []
        if not ops:
            return
        last_w, readers, deps_of = {}, {}, []
        need = set()
        for i, op in enumerate(ops):
            deps = set()
            for b in op.reads:
                if b in last_w:
                    deps.add(last_w[b])
            for b in op.writes:
                if b in last_w:
                    deps.add(last_w[b])
                deps.update(readers.get(b, ()))
            deps.discard(i)
            d2 = set()
            for d in deps:
                p = ops[d]
                if p.eng == "pe" and op.eng == "pe" and p.dma is None and op.dma is None:
                    continue
                d2.add(d)
            deps_of.append(d2)
            need |= d2
            for b in op.reads:
                readers.setdefault(b, []).append(i)
            for b in op.writes:
                last_w[b] = i
                readers[b] = []
        last_of = {}
        for i, op in enumerate(ops):
            if op.dma is not None:
                need.add(i)
            else:
                last_of[op.eng] = i
        need |= set(last_of.values())
        for i, op in enumerate(ops):
            if i not in need:
                continue
            key = ("dma", op.dma) if op.dma is not None else ("eng", op.eng)
            self.counts[key] = self.counts.get(key, 0) + (16 if op.dma is not None else 1)
            op.sig, op.val = key, self.counts[key]
            self._sem(key)
        for i, op in enumerate(ops):
            w = {}
            for d in deps_of[i]:
                p = ops[d]
                if w.get(p.sig, 0) < p.val:
                    w[p.sig] = p.val
            for k, v in w.items():
                if self.waited[op.eng].get(k, 0) >= v:
                    continue
                self.waited[op.eng][k] = v
                op.waits.append((k, v))
        per = {e: [op for op in ops if op.eng == e] for e in ENGS}
        sems, counts, waited = self.sems, dict(self.counts), self.waited

        def run(name, eng):
            for op in per[name]:
                for k, v in op.waits:
                    eng.wait_ge(sems[k], v)
                ins = op.fn(eng)
                if op.sig is not None:
                    ins.then_inc(sems[op.sig], 16 if op.dma is not None else 1)
            for k, v in counts.items():
                if waited[name].get(k, 0) < v:
                    eng.wait_ge(sems[k], v)
                    waited[name][k] = v

        with self.nc.Block() as block:
            @block.sync
            def _(e):
                run("sp", e)

            @block.tensor
            def _(e):
                run("pe", e)

            @block.scalar
            def _(e):
                run("act", e)

            @block.vector
            def _(e):
                run("dve", e)

            @block.gpsimd
            def _(e):
                run("pool", e)


def t5_bucket(rel):
    nb = 16
    ret = (rel > 0).astype(np.int32) * nb
    n = np.abs(rel)
    max_exact = nb // 2
    large = max_exact + (np.log(np.maximum(n, 1) / max_exact)
                         / np.log(1024 / max_exact) * (nb - max_exact)).astype(np.int32)
    large = np.minimum(large, nb - 1)
    return ret + np.where(n < max_exact, n, large).astype(np.int32)


def host_consts():
    bf = ml_dtypes.bfloat16
    c = {}
    c["ident"] = np.eye(128, dtype=np.float32).astype(bf)
    bo = np.zeros((128, 128), np.float32)
    bo[:64, :64] = 1.0 / 64
    bo[64:, 64:] = 1.0 / 64
    c["bones"] = bo.astype(bf)
    c["jflip"] = np.eye(128, dtype=np.float32)[::-1].copy().astype(bf)
    on = np.zeros((128, 2, 128), np.float32)
    on[:, 0, :64] = 1.0
    on[:, 1, 64:] = 1.0
    c["ones2"] = on.astype(bf)
    oh = np.zeros((32, 3, VLEN), np.float32)
    win = np.zeros((8, VLEN), np.float32)
    m = np.arange(383)
    delta = 191 - m
    for g, d in enumerate(DIL):
        bk = t5_bucket(delta * d)
        oh[bk, g, m] = 1.0
    win[:, :383] = (np.abs(delta) <= 64).astype(np.float32)[None, :]
    c["oh"] = oh
    c["win"] = win
    hm = np.zeros((128, 2), np.float32)
    hm[:64, 0] = 1.0
    hm[64:, 1] = 1.0
    c["hmask"] = hm
    return c


def build(debug=None):
    nc = bass.Bass("TRN2", target_bir_lowering=False)
    _uc = [0]

    def uniq(name):
        _uc[0] += 1
        return "t%d_%s" % (_uc[0], name)

    def din(name, shape, dt=F32):
        return nc.dram_tensor(name, list(shape), dt, kind="ExternalInput")

    xs_h = din("xs", [XS_ROWS, DM])
    ng_h = din("norm_g", [2, DM])
    w0_h = din("conv_w_in", [DM, 8192])
    cw_h = din("cw", [128, 16, 3])
    cb_h = din("cb", [128, 16])
    wo0_h = din("conv_w_out", [2048, DM])
    w1_h = din("attn_w_in", [DM, 6144])
    gq_h = din("gqT", [128, 12])
    gk_h = din("gkT", [128, 12])
    wo1_h = din("attn_w_out", [1536, DM])
    tb_h = din("rel_bias_table", [32, 24])
    fl_h = din("fl", [128, 3, 48])
    ident_h = din("ident", [128, 128], BF16)
    bones_h = din("bones", [128, 128], BF16)
    jflip_h = din("jflip", [128, 128], BF16)
    ones2_h = din("ones2", [128, 2, 128], BF16)
    oh_h = din("oh", [32, 3, VLEN])
    win_h = din("win", [8, VLEN])
    hm_h = din("hmask", [128, 2])
    out_h = nc.dram_tensor("out", [OWN, DM], F32, kind="ExternalOutput")

    H1_h = nc.dram_tensor("H1", [H1_ROWS, DM], F32, kind="ExternalOutput") if debug == "A" else nc.dram_tensor("H1", [H1_ROWS, DM], F32)
    H1a_h = nc.dram_tensor("H1a", [H1_ROWS, DM], F32)
    QS_h = nc.dram_tensor("QS", [12, 2, 128, OWN], BF16)
    KS_h = nc.dram_tensor("KS", [12, 128, REG], BF16)
    ZS_h = nc.dram_tensor("ZS", [12, 128, OWN], BF16)
    VS_h = nc.dram_tensor("VS", [12, REG, 128], BF16)
    YS_h = nc.dram_tensor("YS", [12, 128, OWN], BF16)
    VEC_h = nc.dram_tensor("VEC", [24, VLEN], BF16)
    W0S_h = nc.dram_tensor("W0S", [8, 4, 128, 8, 128], BF16)
    WOS_h = nc.dram_tensor("WOS", [8, 128, DM], BF16)
    W1S_h = nc.dram_tensor("W1S", [48, 128, 8, 128], BF16)
    WO1S_h = nc.dram_tensor("WO1S", [12, 128, DM], BF16)

    xs, ng, w0, cw, cb, wo0, w1 = xs_h.ap(), ng_h.ap(), w0_h.ap(), cw_h.ap(), cb_h.ap(), wo0_h.ap(), w1_h.ap()
    wo1, outp, H1 = wo1_h.ap(), out_h.ap(), H1_h.ap()
    H1a = H1a_h.ap()
    QS, KS, ZS, VS, YS, VEC = QS_h.ap(), KS_h.ap(), ZS_h.ap(), VS_h.ap(), YS_h.ap(), VEC_h.ap()
    W0S, WOS, W1S, WO1S = W0S_h.ap(), WOS_h.ap(), W1S_h.ap(), WO1S_h.ap()

    with ExitStack() as gst:
        P = Prog(nc, gst)

        def gsb(name, shape, dt):
            return gst.enter_context(nc.sbuf_tensor(uniq(name), list(shape), dt))

        ident = gsb("ident", [128, 128], BF16)
        bones = gsb("bones", [128, 128], BF16)
        jflip = gsb("jflip", [128, 128], BF16)
        ones2 = gsb("ones2", [128, 2, 128], BF16)
        EB = gsb("EB", [128, 24 * 2 * 128], BF16)
        FL = gsb("FL", [128, 3, 48], F32)
        NB = gsb("NB", [128, 3, 48], F32)
        gqA = gsb("gqA", [128, 12], F32)
        gqB = gsb("gqB", [128, 12], F32)
        gkS = gsb("gkS", [128, 12], F32)
        mh = gsb("mh", [128, 1], F32)
        cwt = gsb("cwt", [128, 16, 3], F32)
        cbt = gsb("cbt", [128, 16], F32)
        EB4 = EB[:, :].rearrange("p (h i q) -> p h i q", h=24, i=2)

        with ExitStack() as st:
            def sb(name, shape, dt):
                return st.enter_context(nc.sbuf_tensor(uniq(name), list(shape), dt))

            def psum(name, shape, dt):
                return st.enter_context(nc.psum_tensor(uniq(name), list(shape), dt))

            tb = sb("tb", [32, 24], F32)
            oh = sb("oh", [32, 3, VLEN], F32)
            win = sb("win", [8, VLEN], F32)
            hm = sb("hm", [128, 2], F32)
            gtmp = sb("gtmp", [128, 12], F32)
            gtmp2 = sb("gtmp2", [128, 12], F32)
            ev = sb("ev", [8, 3, VLEN], F32)
            evb = sb("evb", [8, 3, VLEN], BF16)
            EBp = sb("EBp", [128, 24 * 2 * 128], BF16)
            EBp4 = EBp[:, :].rearrange("p (h i q) -> p h i q", h=24, i=2)
            pV = psum("pV", [128, 512], F32)
            pE = [psum("pE%d" % i, [128, 512], F32) for i in range(2)]

            for nm, t_, h_ in (("ident", ident, ident_h), ("bones", bones, bones_h), ("jflip", jflip, jflip_h),
                               ("ones2", ones2, ones2_h), ("FL", FL, fl_h), ("tb", tb, tb_h), ("oh", oh, oh_h),
                               ("win", win, win_h), ("hm", hm, hm_h), ("cwt", cwt, cw_h), ("cbt", cbt, cb_h),
                               ("gtmp", gtmp, gq_h), ("gtmp2", gtmp2, gk_h)):
                P.add("sp", lambda e, t_=t_, h_=h_: e.dma_start(out=t_[:], in_=h_.ap()), writes=[nm], dma="const")
            P.add("pool", lambda e: e.memset(mh[:], -0.5), writes=["mh"])
            P.add("dve", lambda e: e.tensor_scalar(out=NB[:], in0=FL[:], scalar1=30000.0, scalar2=-30000.0,
                                                   op0=ALU.mult, op1=ALU.add), reads=["FL"], writes=["NB"])
            P.add("dve", lambda e: e.tensor_scalar(out=gqA[:], in0=gtmp[:], scalar1=0.125, scalar2=hm[:, 0:1],
                                                   op0=ALU.mult, op1=ALU.mult), reads=["gtmp", "hm"], writes=["gqA"])
            P.add("dve", lambda e: e.tensor_scalar(out=gqB[:], in0=gtmp[:], scalar1=0.125, scalar2=hm[:, 1:2],
                                                   op0=ALU.mult, op1=ALU.mult), reads=["gtmp", "hm"], writes=["gqB"])
            P.add("dve", lambda e: e.tensor_copy(out=gkS[:], in_=gtmp2[:]), reads=["gtmp2"], writes=["gkS"])
            for g in range(3):
                P.add("pe", lambda e, g=g: e.matmul(pV[0:8, 0:VLEN], lhsT=tb[:, 8 * g:8 * g + 8], rhs=oh[:, g, :],
                                                    start=True, stop=True), reads=["tb", "oh"], writes=["pV"])
                P.add("act", lambda e, g=g: e.activation(out=ev[:, g, :], in_=pV[0:8, 0:VLEN], func=AF.Exp),
                      reads=["pV"], writes=[("ev", g)])
                P.add("dve", lambda e, g=g: e.tensor_tensor(out=evb[:, g, :], in0=ev[:, g, :], in1=win[:], op=ALU.mult),
                      reads=[("ev", g), "win"], writes=[("evb", g)])
                P.add("sp", lambda e, g=g: e.dma_start(out=VEC[8 * g:8 * g + 8, :], in_=evb[:, g, :]),
                      reads=[("evb", g)], writes=["VEC"], dma="const")
            for i in range(2):
                src = bass.AP(VEC_h, 128 - 128 * i, [[1, 128], [VLEN, 24], [1, 128]])
                P.add("sp", lambda e, i=i, src=src: e.dma_start(out=EBp4[:, :, i, :], in_=src),
                      reads=["VEC"], writes=["EBp"], dma="const")
            for n in range(12):
                pe_ = pE[n % 2]
                P.add("pe", lambda e, n=n, pe_=pe_: e.matmul(pe_[:, :], lhsT=jflip[:], rhs=EBp[:, 512 * n:512 * n + 512],
                                                             start=True, stop=True),
                      reads=["jflip", "EBp"], writes=[("pE", n % 2)])
                P.add("act" if n % 2 else "dve",
                      (lambda e, n=n, pe_=pe_: e.activation(out=EB[:, 512 * n:512 * n + 512], in_=pe_[:, :], func=AF.Copy)) if n % 2
                      else (lambda e, n=n, pe_=pe_: e.tensor_copy(out=EB[:, 512 * n:512 * n + 512], in_=pe_[:, :])),
                      reads=[("pE", n % 2)], writes=[("EB", n)])
            P.flush()

        MS = (128, 128, 128, WA - 384)
        for pa in range(2):
            with ExitStack() as st:
                def sb(name, shape, dt):
                    return st.enter_context(nc.sbuf_tensor(uniq(name), list(shape), dt))

                def psum(name, shape, dt):
                    return st.enter_context(nc.psum_tensor(uniq(name), list(shape), dt))

                NWS = 10
                W0 = sb("W0", [128, 8, 4, 8, 128], BF16)
                WO = sb("WO", [128, 8, DM], BF16)
                wst = [sb("wst%d" % i, [128, 512], F32) for i in range(NWS)]
                gb = sb("gb", [128, DM], F32)
                xa = [sb("xa%d" % i, [128, DM], F32) for i in range(4)]
                xr = [sb("xr%d" % i, [128, DM], F32) for i in range(4)]
                xn = [sb("xn%d" % i, [128, DM], BF16) for i in range(4)]
                ss = [sb("ss%d" % i, [128, 1], F32) for i in range(4)]
                vv = [sb("vv%d" % i, [128, 1], F32) for i in range(4)]
                rs = [sb("rs%d" % i, [128, 1], F32) for i in range(4)]
                xnT = [sb("xnT%d" % i, [128, 8, WA], BF16) for i in range(2)]
                ybuf = sb("ybuf", [128, 8, WA], BF16)
                usb = [sb("usb%d" % i, [128, WA], F32) for i in range(2)]
                cgu = [sb("cgu%d" % i, [128, WA], F32) for i in range(2)]
                szb = [sb("szb%d" % i, [128, WA], BF16) for i in range(2)]
                gat = [sb("gat%d" % i, [128, WA], F32) for i in range(2)]
                tcv = [sb("tcv%d" % i, [128, WA], F32) for i in range(2)]
                NPJ = 4
                pj = [psum("pj%d" % i, [128, 512], F32) for i in range(NPJ)]
                pT = [psum("pT%d" % i, [128, 8, 128], BF16) for i in range(2)]
                po = [psum("po%d" % i, [128, 512], F32) for i in range(2)]

                P.add("sp", lambda e: e.dma_start(out=gb[:], in_=ng[0:1, :].partition_broadcast(128)),
                      writes=["gb"], dma="gb")
                P.add("pool", lambda e: e.memset(ybuf[:], 0.0), writes=["ybuf"])
                wcnt = [0]

                def load_w0(el):
                    e_ = 8 * pa + el
                    if pa == 1:
                        for part in (2, 1, 3, 0):
                            P.add("sp", lambda e, part=part, el=el: e.dma_start(out=W0[:, :, part, el, :], in_=W0S[el, part]),
                                  writes=[("W0", part, el, 0), ("W0", part, el, 1)], dma=("w0d", (4 * el + part) % 4))
                        return
                    if DIRECT_CAST:
                        for part in (2, 1, 3, 0):
                            c0 = part * 2048 + e_ * 128
                            P.add("pool", lambda e, part=part, el=el, c0=c0: e.dma_start(
                                out=W0[:, :, part, el, :], in_=w0[:, c0:c0 + 128].rearrange("(k p) c -> p k c", p=128)),
                                writes=[("W0", part, el, 0), ("W0", part, el, 1)], dma=("w0d", (4 * el + part) % 4))
                        return
                    for part in (2, 1, 3, 0):
                        c0 = part * 2048 + e_ * 128
                        for kh in range(2):
                            sl = wcnt[0] % NWS
                            wcnt[0] += 1
                            P.add("sp", lambda e, sl=sl, c0=c0, kh=kh: e.dma_start(
                                out=wst[sl][:, :].rearrange("p (k c) -> p k c", k=4),
                                in_=w0[kh * 512:(kh + 1) * 512, c0:c0 + 128].rearrange("(k p) c -> p k c", p=128)),
                                writes=[("wst", sl)], dma=("wst", sl))
                            ceng = ("act", "dve", "act", "dve", "pool")[wcnt[0] % 5]
                            if ceng == "act":
                                P.add("act", lambda e, sl=sl, part=part, el=el, kh=kh: e.activation(
                                    out=W0[:, 4 * kh:4 * kh + 4, part, el, :],
                                    in_=wst[sl][:, :].rearrange("p (k c) -> p k c", k=4), func=AF.Copy),
                                    reads=[("wst", sl)], writes=[("W0", part, el, kh)])
                            else:
                                P.add(ceng, lambda e, sl=sl, part=part, el=el, kh=kh: e.tensor_copy(
                                    out=W0[:, 4 * kh:4 * kh + 4, part, el, :],
                                    in_=wst[sl][:, :].rearrange("p (k c) -> p k c", k=4)),
                                    reads=[("wst", sl)], writes=[("W0", part, el, kh)])

                def load_wo(el):
                    e_ = 8 * pa + el
                    if pa == 1:
                        P.add("sp", lambda e, el=el: e.dma_start(out=WO[:, el, :], in_=WOS[el]),
                              writes=[("WO", el, 0), ("WO", el, 1)], dma=("wod", el % 4))
                        return
                    if DIRECT_CAST:
                        P.add("pool", lambda e, el=el, e_=e_: e.dma_start(out=WO[:, el, :], in_=wo0[e_ * 128:(e_ + 1) * 128, :]),
                              writes=[("WO", el, 0), ("WO", el, 1)], dma=("wod", el % 4))
                        return
                    for hf in range(2):
                        sl = wcnt[0] % NWS
                        wcnt[0] += 1
                        P.add("sp", lambda e, sl=sl, e_=e_, hf=hf: e.dma_start(
                            out=wst[sl][:, :], in_=wo0[e_ * 128:(e_ + 1) * 128, hf * 512:(hf + 1) * 512]),
                            writes=[("wst", sl)], dma=("wst", sl))
                        P.add("pool", lambda e, sl=sl, el=el, hf=hf: e.tensor_copy(out=WO[:, el, hf * 512:(hf + 1) * 512], in_=wst[sl][:, :]),
                              reads=[("wst", sl)], writes=[("WO", el, hf)])

                bg = []
                if pa == 0:
                    for el in range(8):
                        for part in (2, 1, 3, 0):
                            c0 = part * 2048 + (8 + el) * 128
                            bg.append((W0S[el, part], w0[:, c0:c0 + 128].rearrange("(k p) c -> p k c", p=128)))
                        bg.append((WOS[el], wo0[(8 + el) * 128:(9 + el) * 128, :]))
                    for n_ in range(48):
                        bg.append((W1S[n_], w1[:, 128 * n_:128 * n_ + 128].rearrange("(k p) c -> p k c", p=128)))
                    for c_ in range(12):
                        bg.append((WO1S[c_], wo1[c_ * 128:(c_ + 1) * 128, :]))
                bgc = [0]

                def bg_cast(n):
                    for _ in range(n):
                        if bg:
                            dst, src = bg.pop(0)
                            P.add("pool", lambda e, dst=dst, src=src: e.dma_start(out=dst, in_=src),
                                  dma=("bg", bgc[0] % 4))
                            bgc[0] += 1

                scnt = [0]
                pjc = [0]
                poc = [0]

                def prep_dma(t):
                    for s in range(4):
                        M = MS[s]
                        r0 = TA * t + 128 * s
                        P.add("sp", lambda e, s=s, r0=r0, M=M: e.dma_start(out=xa[s][0:M, :], in_=xs[r0:r0 + M, :]),
                              writes=[("xa", s)], dma=("xa", s))

                def prep_sub(t, s):
                    M = MS[s]
                    sl = s
                    P.add("act", lambda e, sl=sl, s=s, M=M: e.activation(out=xn[s][0:M, :], in_=xa[sl][0:M, :],
                                                                        func=AF.Square, accum_out=ss[sl][0:M, :]),
                          reads=[("xa", sl)], writes=[("xn", s), ("ss", sl)])
                    P.add("dve", lambda e, sl=sl, M=M: e.tensor_scalar(out=vv[sl][0:M, :], in0=ss[sl][0:M, :],
                                                                      scalar1=1.0 / DM, scalar2=EPS,
                                                                      op0=ALU.mult, op1=ALU.add),
                          reads=[("ss", sl)], writes=[("vv", sl)])
                    P.add("pool", lambda e, sl=sl, M=M: e.tensor_tensor(out=rs[sl][0:M, :], in0=vv[sl][0:M, :],
                                                                       in1=mh[0:M, :], op=ALU.pow),
                          reads=[("vv", sl), "mh"], writes=[("rs", sl)])
                    P.add("dve", lambda e, sl=sl, s=s, M=M: e.scalar_tensor_tensor(
                        out=xn[s][0:M, :], in0=xa[sl][0:M, :], scalar=rs[sl][0:M, :], in1=gb[0:M, :],
                        op0=ALU.mult, op1=ALU.mult),
                        reads=[("xa", sl), ("rs", sl), "gb"], writes=[("xn", s)])

                def prep_load(t):
                    prep_dma(t)
                    for s in range(4):
                        prep_sub(t, s)

                def prep_T(t):
                    xi = t % 2
                    for s in range(4):
                        M = MS[s]
                        pi = s % 2
                        for k in range(8):
                            P.add("pe", lambda e, s=s, k=k, M=M, pi=pi: e.transpose(out=pT[pi][:, k, 0:M],
                                                                                    in_=xn[s][0:M, k * 128:(k + 1) * 128],
                                                                                    identity=ident[0:M, 0:M]),
                                  reads=[("xn", s), "ident"], writes=[("pT", pi)])
                        if s % 2:
                            P.add("act", lambda e, s=s, M=M, pi=pi, xi=xi: e.activation(
                                out=xnT[xi][:, :, 128 * s:128 * s + M], in_=pT[pi][:, :, 0:M], func=AF.Copy),
                                reads=[("pT", pi)], writes=[("xnT", xi)])
                        else:
                            P.add("dve", lambda e, s=s, M=M, pi=pi, xi=xi: e.tensor_copy(
                                out=xnT[xi][:, :, 128 * s:128 * s + M], in_=pT[pi][:, :, 0:M]),
                                reads=[("pT", pi)], writes=[("xnT", xi)])

                def phase1(t, nxt=None):
                    xi = t % 2
                    for el in range(8):
                        if t == 0 and pa == 0 and DIRECT_CAST:
                            if el + 2 < 8:
                                load_w0(el + 2)
                        elif t == 0:
                            load_w0(el)
                        if nxt is not None and el % 2 == 1:
                            prep_sub(nxt, el // 2)
                        e_ = 8 * pa + el
                        st_ = el % 2
                        bank = {}
                        for part in (2, 1, 3, 0):
                            bi = pjc[0] % NPJ
                            pjc[0] += 1
                            bank[part] = bi
                            for k in range(8):
                                P.add("pe", lambda e, bi=bi, k=k, part=part, el=el, xi=xi: e.matmul(
                                    pj[bi][:, 0:WA], lhsT=W0[:, k, part, el, :], rhs=xnT[xi][:, k, :],
                                    start=(k == 0), stop=(k == 7)),
                                    reads=[("W0", part, el, k // 4), ("xnT", xi)], writes=[("pj", bi)])
                        bB, bC, bU, bZ = bank[0], bank[1], bank[2], bank[3]
                        P.add("act", lambda e, st_=st_, bU=bU: e.activation(out=usb[st_][:, :], in_=pj[bU][:, 0:WA], func=AF.Copy),
                              reads=[("pj", bU)], writes=[("usb", st_)])
                        P.add("dve", lambda e, st_=st_, bC=bC: e.tensor_tensor(out=cgu[st_][:, :], in0=pj[bC][:, 0:WA],
                                                                             in1=usb[st_][:, :], op=ALU.mult),
                              reads=[("pj", bC), ("usb", st_)], writes=[("cgu", st_)])
                        P.add("act", lambda e, st_=st_, e_=e_: e.activation(
                            out=tcv[st_][:, 1:WA - 1], in_=cgu[st_][:, 1:WA - 1], func=AF.Identity,
                            scale=cwt[:, e_, 1:2], bias=cbt[:, e_:e_ + 1]),
                            reads=[("cgu", st_), "cwt", "cbt"], writes=[("tcv", st_)])
                        P.add("act", lambda e, st_=st_, bZ=bZ: e.activation(out=szb[st_][:, :], in_=pj[bZ][:, 0:WA], func=AF.Silu),
                              reads=[("pj", bZ)], writes=[("szb", st_)])
                        P.add("dve", lambda e, st_=st_, bB=bB: e.tensor_tensor(out=gat[st_][:, :], in0=pj[bB][:, 0:WA],
                                                                             in1=szb[st_][:, :], op=ALU.mult),
                              reads=[("pj", bB), ("szb", st_)], writes=[("gat", st_)])
                        P.add("dve", lambda e, st_=st_, e_=e_: e.scalar_tensor_tensor(
                            out=tcv[st_][:, 1:WA - 1], in0=cgu[st_][:, 0:WA - 2], scalar=cwt[:, e_, 0:1],
                            in1=tcv[st_][:, 1:WA - 1], op0=ALU.mult, op1=ALU.add),
                            reads=[("cgu", st_), ("tcv", st_), "cwt"], writes=[("tcv", st_)])
                        P.add("dve", lambda e, st_=st_, e_=e_: e.scalar_tensor_tensor(
                            out=tcv[st_][:, 1:WA - 1], in0=cgu[st_][:, 2:WA], scalar=cwt[:, e_, 2:3],
                            in1=tcv[st_][:, 1:WA - 1], op0=ALU.mult, op1=ALU.add),
                            reads=[("cgu", st_), ("tcv", st_), "cwt"], writes=[("tcv", st_)])
                        P.add("pool", lambda e, st_=st_, el=el: e.tensor_tensor(
                            out=ybuf[:, el, 1:WA - 1], in0=tcv[st_][:, 1:WA - 1], in1=gat[st_][:, 1:WA - 1], op=ALU.mult),
                            reads=[("tcv", st_), ("gat", st_)], writes=[("ybuf", el)])
                        if t == 0:
                            load_wo(el)
                        elif t >= 2:
                            bg_cast(2)

                def xr_dma(t):
                    for s in range(4):
                        M = MS[s]
                        plo = 1 if s == 0 else 0
                        phi = min(M, TA + 1 - 128 * s)
                        if t == 0 and s == 0:
                            plo = 0
                        if t == NTA - 1 and s == 3:
                            phi = M
                        jlo = TA * t + 128 * s + plo - 1
                        n = phi - plo
                        if pa == 0:
                            r0 = TA * t + 128 * s
                            P.add("sp", lambda e, s=s, r0=r0, M=M: e.dma_start(out=xr[s][0:M, :], in_=xs[r0:r0 + M, :]),
                                  writes=[("xr", s)], dma=("xr", s))
                        else:
                            r0 = TA * t + 128 * s
                            P.add("sp", lambda e, s=s, r0=r0, M=M: e.dma_start(out=xr[s][0:M, :], in_=H1a[r0:r0 + M, :]),
                                  writes=[("xr", s)], dma=("xr", s))

                def phase2(t):
                    for s in range(4):
                        M = MS[s]
                        sl = s
                        plo = 1 if s == 0 else 0
                        phi = min(M, TA + 1 - 128 * s)
                        if t == 0 and s == 0:
                            plo = 0
                        if t == NTA - 1 and s == 3:
                            phi = M
                        jlo = TA * t + 128 * s + plo - 1
                        n = phi - plo
                        for half in range(2):
                            bi = poc[0] % 2
                            poc[0] += 1
                            for el in range(8):
                                P.add("pe", lambda e, bi=bi, el=el, s=s, M=M, half=half: e.matmul(
                                    po[bi][0:M, :], lhsT=ybuf[:, el, 128 * s:128 * s + M],
                                    rhs=WO[:, el, half * 512:(half + 1) * 512], start=(el == 0), stop=(el == 7)),
                                    reads=[("ybuf", el), ("WO", el, half)], writes=[("po", bi)])
                            P.add("dve", lambda e, bi=bi, sl=sl, M=M, half=half: e.tensor_tensor(
                                out=xr[sl][0:M, half * 512:(half + 1) * 512], in0=po[bi][0:M, :],
                                in1=xr[sl][0:M, half * 512:(half + 1) * 512], op=ALU.add),
                                reads=[("po", bi), ("xr", sl)], writes=[("xr", sl)])
                        r0 = TA * t + 128 * s
                        n1 = (n // 16) * 16
                        for pi_, (a_, b_) in enumerate(((plo, plo + n1), (plo + n1, plo + n))):
                            if b_ > a_:
                                P.add("pool", lambda e, sl=sl, a_=a_, b_=b_, r0=r0: e.dma_start(
                                    out=(H1a if pa == 0 else H1)[r0 + a_:r0 + b_, :], in_=xr[sl][a_:b_, :]),
                                    reads=[("xr", sl)], writes=[("H1", t, s, pi_)], dma=("h1w", sl, pi_))

                if pa == 0 and DIRECT_CAST:
                    load_w0(0)
                    load_w0(1)
                prep_load(0)
                prep_T(0)
                for t in range(NTA):
                    xr_dma(t)
                    if t + 1 < NTA:
                        prep_dma(t + 1)
                    phase1(t, t + 1 if t + 1 < NTA else None)
                    if t + 1 < NTA:
                        prep_T(t + 1)
                    phase2(t)
                P.flush()

        if debug == "A":
            return nc

        with ExitStack() as st:
            def sb(name, shape, dt):
                return st.enter_context(nc.sbuf_tensor(uniq(name), list(shape), dt))

            def psum(name, shape, dt):
                return st.enter_context(nc.psum_tensor(uniq(name), list(shape), dt))

            W1 = sb("W1", [128, 8, 6144], BF16)
            wst = [sb("wstb%d" % i, [128, 1024], F32) for i in range(6)]
            gb = sb("gb1", [128, DM], F32)
            xa = [sb("xab%d" % i, [128, DM], F32) for i in range(4)]
            xn = [sb("xnb%d" % i, [128, DM], BF16) for i in range(4)]
            ss = [sb("ssb%d" % i, [128, 1], F32) for i in range(4)]
            vv = [sb("vvb%d" % i, [128, 1], F32) for i in range(4)]
            rs = [sb("rsb%d" % i, [128, 1], F32) for i in range(4)]
            hnTs = [sb("hnT%d" % i, [128, 8, 512], BF16) for i in range(2)]
            cur = {"h": 0}
            sq = [sb("sq%d" % i, [128, 512], BF16) for i in range(2)]
            lnv = [sb("lnv%d" % i, [128, 512], F32) for i in range(2)]
            rq = [sb("rq%d" % i, [128, 512], F32) for i in range(2)]
            qa = [sb("qa%d" % i, [128, 512], BF16) for i in range(2)]
            qb = [sb("qb%d" % i, [128, 512], BF16) for i in range(2)]
            kk = [sb("kk%d" % i, [128, 512], BF16) for i in range(2)]
            e1 = [sb("e1%d" % i, [128, 512], F32) for i in range(2)]
            zo = [sb("zo%d" % i, [128, 512], BF16) for i in range(2)]
            vo = [sb("vo%d" % i, [128, 512], BF16) for i in range(2)]
            pj = [psum("pjb%d" % i, [128, 512], F32) for i in range(5)]
            pn = [psum("pnb%d" % i, [128, 512], F32) for i in range(2)]
            pT = psum("pTb", [128, 8, 128], BF16)

            P.add("sp", lambda e: e.dma_start(out=gb[:], in_=ng[1:2, :].partition_broadcast(128)),
                  writes=["gb"], dma="gb")
            col_order = []
            for g in (2, 0, 1):
                col_order += [g * 1536 + 512 + hp * 128 for hp in range(4)]
                col_order += [g * 1536 + 1024 + i * 128 for i in range(4)]
            for g in range(3):
                for hp in range(4):
                    col_order += [g * 1536 + hp * 128, 4608 + g * 512 + hp * 128]
            assert len(set(col_order)) == 48
            wloaded = set()
            wn = [0]

            def ensure_w(c0):
                if c0 in wloaded:
                    return
                wloaded.add(c0)
                n = wn[0]
                wn[0] += 1
                P.add("sp", lambda e, c0=c0: e.dma_start(out=W1[:, :, c0:c0 + 128], in_=W1S[c0 // 128]),
                      writes=[("W1", c0 // 128)], dma=("w1d", n % 4))

            scnt = 0
            pjc = [0]
            pnc = [0]
            cnt = {"q": 0, "k": 0, "z": 0, "v": 0}

            NPJB = 5

            def proj(c0):
                ensure_w(c0)
                bi = pjc[0] % NPJB
                pjc[0] += 1
                for k in range(8):
                    hi = cur["h"]
                    P.add("pe", lambda e, bi=bi, k=k, c0=c0, hi=hi: e.matmul(pj[bi][:, :], lhsT=W1[:, k, c0:c0 + 128],
                                                                             rhs=hnTs[hi][:, k, :], start=(k == 0), stop=(k == 7)),
                          reads=[("W1", c0 // 128), ("hnT", hi)], writes=[("pj", bi)])
                return bi

            def item_qk(kind, c, c0, pos):
                st = {}

                def s1():
                    bi = proj(c0)
                    ni = pnc[0] % 2
                    pnc[0] += 1
                    st["bi"], st["ni"] = bi, ni
                    P.add("act", lambda e: e.activation(out=sq[ni][:, :], in_=pj[bi][:, :], func=AF.Square),
                          reads=[("pj", bi)], writes=[("sq", ni)])

                def s2():
                    bi, ni = st["bi"], st["ni"]
                    P.add("pe", lambda e: e.matmul(pn[ni][:, :], lhsT=bones[:], rhs=sq[ni][:, :], start=True, stop=True),
                          reads=["bones", ("sq", ni)], writes=[("pn", ni)])
                    P.add("act", lambda e: e.activation(out=lnv[ni][:, :], in_=pn[ni][:, :], func=AF.Ln, bias=EPS),
                          reads=[("pn", ni)], writes=[("lnv", ni)])
                    P.add("act", lambda e: e.activation(out=rq[ni][:, :], in_=lnv[ni][:, :], func=AF.Exp, scale=-0.5),
                          reads=[("lnv", ni)], writes=[("rq", ni)])
                    if kind == "q":
                        qi = cnt["q"] % 2
                        cnt["q"] += 1
                        P.add("dve", lambda e: e.scalar_tensor_tensor(
                            out=qa[qi][:, :], in0=pj[bi][:, :], scalar=gqA[:, c:c + 1], in1=rq[ni][:, :],
                            op0=ALU.mult, op1=ALU.mult),
                            reads=[("pj", bi), ("rq", ni), "gqA"], writes=[("qa", qi)])
                        P.add("dve", lambda e: e.scalar_tensor_tensor(
                            out=qb[qi][:, :], in0=pj[bi][:, :], scalar=gqB[:, c:c + 1], in1=rq[ni][:, :],
                            op0=ALU.mult, op1=ALU.mult),
                            reads=[("pj", bi), ("rq", ni), "gqB"], writes=[("qb", qi)])
                        P.add("pool", lambda e: e.dma_start(out=QS[c, 0, :, pos:pos + 512], in_=qa[qi][:, :]),
                              reads=[("qa", qi)], writes=[("QSa", c, pos)], dma=("qa", qi))
                        P.add("pool", lambda e: e.dma_start(out=QS[c, 1, :, pos:pos + 512], in_=qb[qi][:, :]),
                              reads=[("qb", qi)], writes=[("QSb", c, pos)], dma=("qb", qi))
                    else:
                        ki = cnt["k"] % 2
                        cnt["k"] += 1
                        P.add("dve", lambda e: e.scalar_tensor_tensor(
                            out=kk[ki][:, :], in0=pj[bi][:, :], scalar=gkS[:, c:c + 1], in1=rq[ni][:, :],
                            op0=ALU.mult, op1=ALU.mult),
                            reads=[("pj", bi), ("rq", ni), "gkS"], writes=[("kk", ki)])
                        P.add("pool", lambda e: e.dma_start(out=KS[c, :, pos:pos + 512], in_=kk[ki][:, :]),
                              reads=[("kk", ki)], writes=[("KS", c, pos)], dma=("kk", ki))
                return s1, s2

            def item_z(c, c0, pos):
                st = {}

                def s1():
                    bi = proj(c0)
                    zi = cnt["z"] % 2
                    cnt["z"] += 1
                    st["bi"], st["zi"] = bi, zi
                    P.add("act", lambda e: e.activation(out=e1[zi][:, :], in_=pj[bi][:, :], func=AF.Exp, scale=-1.0),
                          reads=[("pj", bi)], writes=[("e1", zi)])

                def s2():
                    bi, zi = st["bi"], st["zi"]
                    P.add("act", lambda e: e.activation(out=e1[zi][:, :], in_=e1[zi][:, :], func=AF.Ln, bias=1.0),
                          reads=[("e1", zi)], writes=[("e1", zi)])
                    P.add("act", lambda e: e.activation(out=e1[zi][:, :], in_=e1[zi][:, :], func=AF.Exp, scale=-1.0),
                          reads=[("e1", zi)], writes=[("e1", zi)])
                    P.add("dve", lambda e: e.tensor_tensor(out=zo[zi][:, :], in0=pj[bi][:, :], in1=e1[zi][:, :], op=ALU.mult),
                          reads=[("pj", bi), ("e1", zi)], writes=[("zo", zi)])
                    P.add("pool", lambda e: e.dma_start(out=ZS[c, :, pos:pos + 512], in_=zo[zi][:, :]),
                          reads=[("zo", zi)], writes=[("ZS", c, pos)], dma=("zo", zi))
                return s1, s2

            def item_v(g, s_, r0):
                st = {}

                def s1():
                    bi = pjc[0] % NPJB
                    pjc[0] += 1
                    st["bi"] = bi
                    vc0 = g * 1536 + 1024
                    for i_ in range(4):
                        ensure_w(vc0 + 128 * i_)
                    hi = cur["h"]
                    for k in range(8):
                        P.add("pe", lambda e, k=k, hi=hi: e.matmul(
                            pj[bi][:, :], lhsT=hnTs[hi][:, k, 128 * s_:128 * s_ + 128], rhs=W1[:, k, vc0:vc0 + 512],
                            start=(k == 0), stop=(k == 7)),
                            reads=[("hnT", hi)] + [("W1", vc0 // 128 + i) for i in range(4)], writes=[("pj", bi)])

                def s2():
                    bi = st["bi"]
                    vi = cnt["v"] % 2
                    cnt["v"] += 1
                    if vi:
                        P.add("act", lambda e: e.activation(out=vo[vi][:, :], in_=pj[bi][:, :], func=AF.Copy),
                              reads=[("pj", bi)], writes=[("vo", vi)])
                    else:
                        P.add("dve", lambda e: e.tensor_copy(out=vo[vi][:, :], in_=pj[bi][:, :]),
                              reads=[("pj", bi)], writes=[("vo", vi)])
                    P.add("pool", lambda e: e.dma_start(
                        out=VS[4 * g:4 * g + 4, r0:r0 + 128, :].rearrange("h t c -> t h c"),
                        in_=vo[vi][:, :].rearrange("t (h c) -> t h c", h=4)),
                        reads=[("vo", vi)], writes=[("VS", g, r0)], dma=("vo", vi))
                return s1, s2

            def prep_dma_B(tt):
                j0 = 512 * tt
                for s in range(4):
                    r0 = j0 + 128 * s
                    P.add("sp", lambda e, s=s, r0=r0: e.dma_start(out=xa[s][:, :], in_=H1[r0 + 1:r0 + 129, :]),
                          writes=[("xa", s)], dma=("xa", s))

            def prep_sub_B(tt, s):
                sl = s
                P.add("act", lambda e, sl=sl, s=s: e.activation(out=xn[s][:, :], in_=xa[sl][:, :], func=AF.Square,
                                                               accum_out=ss[sl][:, :]),
                      reads=[("xa", sl)], writes=[("xn", s), ("ss", sl)])
                P.add("dve", lambda e, sl=sl: e.tensor_scalar(out=vv[sl][:, :], in0=ss[sl][:, :], scalar1=1.0 / DM,
                                                             scalar2=EPS, op0=ALU.mult, op1=ALU.add),
                      reads=[("ss", sl)], writes=[("vv", sl)])
                P.add("pool", lambda e, sl=sl: e.tensor_tensor(out=rs[sl][:, :], in0=vv[sl][:, :], in1=mh[:, :], op=ALU.pow),
                      reads=[("vv", sl), "mh"], writes=[("rs", sl)])
                P.add("dve", lambda e, sl=sl, s=s: e.scalar_tensor_tensor(
                    out=xn[s][:, :], in0=xa[sl][:, :], scalar=rs[sl][:, :], in1=gb[:, :], op0=ALU.mult, op1=ALU.mult),
                    reads=[("xa", sl), ("rs", sl), "gb"], writes=[("xn", s)])

            def prep_load_B(tt):
                prep_dma_B(tt)
                for s in range(4):
                    prep_sub_B(tt, s)

            def prep_T_B(tt, subs=(0, 1, 2, 3)):
                hi = tt % 2
                for s in subs:
                    for k in range(8):
                        P.add("pe", lambda e, s=s, k=k: e.transpose(out=pT[:, k, :], in_=xn[s][:, k * 128:(k + 1) * 128],
                                                                    identity=ident[:, :]),
                              reads=[("xn", s), "ident"], writes=["pT"])
                    if s % 2:
                        P.add("act", lambda e, s=s, hi=hi: e.activation(out=hnTs[hi][:, :, 128 * s:128 * s + 128], in_=pT[:, :, :], func=AF.Copy),
                              reads=["pT"], writes=[("hnT", hi)])
                    else:
                        P.add("dve", lambda e, s=s, hi=hi: e.tensor_copy(out=hnTs[hi][:, :, 128 * s:128 * s + 128], in_=pT[:, :, :]),
                              reads=["pT"], writes=[("hnT", hi)])

            pending = []
            pf = []
            for g in range(3):
                for hp in range(4):
                    pf += [g * 1536 + hp * 128, 4608 + g * 512 + hp * 128]
            prep_load_B(0)
            prep_T_B(0)
            for tt in range(12):
                j0 = 512 * tt
                own = 2 <= tt < 10
                kv01 = 1 <= tt <= 10
                cur["h"] = tt % 2
                if tt + 1 < 12:
                    prep_dma_B(tt + 1)
                items = []
                for g in range(3):
                    for hp in range(4):
                        c = g * 4 + hp
                        if own:
                            items.append(item_qk("q", c, g * 1536 + hp * 128, (tt - 2) * 512))
                        if g == 2 or kv01:
                            items.append(item_qk("k", c, g * 1536 + 512 + hp * 128, j0))
                        if own:
                            items.append(item_z(c, 4608 + g * 512 + hp * 128, (tt - 2) * 512))
                for g in range(3):
                    if g == 2 or kv01:
                        for s in range(4):
                            items.append(item_v(g, s, j0 + 128 * s))
                for n, (s1, s2) in enumerate(items):
                    s1()
                    if tt >= 1 and pf:
                        ensure_w(pf.pop(0))
                    for p2 in pending:
                        p2()
                    pending = [s2]
                    if tt + 1 < 12:
                        L_ = len(items)
                        for s_ in range(4):
                            if n == (s_ + 1) * L_ // 10:
                                prep_sub_B(tt + 1, s_)
                        sp_ = max(1, (L_ - L_ // 2 - 1) // 4)
                        for s_ in range(4):
                            if n == L_ // 2 + sp_ * s_:
                                prep_T_B(tt + 1, (s_,))
            for s2 in pending:
                s2()
            P.flush()

        if debug == "B":
            return nc

        with ExitStack() as st:
            def sb(name, shape, dt):
                return st.enter_context(nc.sbuf_tensor(uniq(name), list(shape), dt))

            def psum(name, shape, dt):
                return st.enter_context(nc.psum_tensor(uniq(name), list(shape), dt))

            QA = [sb("QA%d" % i, [128, OWN], BF16) for i in range(2)]
            QB = [sb("QB%d" % i, [128, OWN], BF16) for i in range(2)]
            KT = [sb("KT%d" % i, [128, REG], BF16) for i in range(2)]
            ZT = [sb("ZT%d" % i, [128, OWN], BF16) for i in range(2)]
            Vb = sb("Vb", [128, 48, 2, 128], BF16)
            Dn = sb("Dn", [128, OWN], F32)
            U = [sb("U%d" % i, [128, OWN], F32) for i in range(3)]
            yb = [sb("yb%d" % i, [128, OWN], BF16) for i in range(2)]
            pex = [sb("pex%d" % i, [128, 2, 2, 128], BF16) for i in range(4)]
            pm = [sb("pm%d" % i, [128, 2, 2, 128], BF16) for i in range(4)]
            pS = [psum("pS%d" % i, [128, 2, 2, 128], F32) for i in range(4)]
            pO = [psum("pO%d" % i, [128, 128], F32) for i in range(2)]
            pD = [psum("pD%d" % i, [128, 128], F32) for i in range(2)]

            P.add("pool", lambda e: e.memset(Vb[:], 0.0), writes=["Vbz"])
            cc = 0
            bc = 0
            yc = 0
            for hp in range(4):
                for g in range(3):
                    c = g * 4 + hp
                    d = DIL[g]
                    Lk = OWN + 128 * d
                    reg0 = HALO - 64 * d
                    nti = 32 // d + 1
                    nb = 32 // d
                    st_ = cc % 2
                    cc += 1
                    P.add("sp", lambda e, st_=st_, c=c: e.dma_start(out=QA[st_][:, :], in_=QS[c, 0, :, :]),
                          writes=[("QA", st_)], dma=("QA", st_))
                    P.add("sp", lambda e, st_=st_, c=c: e.dma_start(out=QB[st_][:, :], in_=QS[c, 1, :, :]),
                          writes=[("QB", st_)], dma=("QB", st_))
                    P.add("sp", lambda e, st_=st_, c=c, reg0=reg0, Lk=Lk: e.dma_start(out=KT[st_][:, 0:Lk], in_=KS[c, :, reg0:reg0 + Lk]),
                          writes=[("KT", st_)], dma=("KT", st_))
                    P.add("sp", lambda e, st_=st_, c=c: e.dma_start(out=ZT[st_][:, :], in_=ZS[c, :, :]),
                          writes=[("ZT", st_)], dma=("ZT", st_))
                    nv = 0
                    for r in range(d):
                        for i0 in range(0, nti, 8):
                            ni_ = min(8, nti - i0)
                            for hh in range(2):
                                off = (c * REG + reg0 + r + d * 128 * i0) * 128 + 64 * hh
                                src = bass.AP(VS_h, off, [[d * 128, 128], [128 * d * 128, ni_], [1, 64]])
                                n0 = r * nti + i0
                                P.add("sp", lambda e, src=src, n0=n0, ni_=ni_, hh=hh: e.dma_start(
                                    out=Vb[:, n0:n0 + ni_, hh, 64 * hh:64 * hh + 64], in_=src),
                                    reads=["Vbz"], writes=[("Vb", hh, n_) for n_ in range(n0, n0 + ni_)], dma=("Vb", nv % 12))
                                nv += 1
                    def stageS(r, b, bi, st_=st_, d=d, nti=nti, g=g, hp=hp):
                        q0 = r + 128 * b * d
                        qsl = slice(q0, q0 + 127 * d + 1, d) if d > 1 else slice(q0, q0 + 128)
                        for hh in range(2):
                            Qt = QA if hh == 0 else QB
                            for i in range(2):
                                k0 = r + 128 * (b + i) * d
                                ksl = slice(k0, k0 + 127 * d + 1, d) if d > 1 else slice(k0, k0 + 128)
                                P.add("pe", lambda e, hh=hh, i=i, ksl=ksl, Qt=Qt: e.matmul(
                                    pS[bi][:, hh, i, :], lhsT=KT[st_][:, ksl], rhs=Qt[st_][:, qsl], start=True, stop=True),
                                    reads=[("KT", st_), ("QA", st_), ("QB", st_)], writes=[("pS", bi)])
                        for i in range(2):
                            tn = r * nti + b + i
                            P.add("act", lambda e, i=i, tn=tn: e.activation(out=pex[bi][:, :, i, :], in_=pS[bi][:, :, i, :], func=AF.Exp,
                                                                            bias=NB[:, g, tn:tn + 1]),
                                  reads=[("pS", bi), "NB"], writes=[("pex", bi, i)])
                        meng = "pool" if bi == 3 else "dve"
                        P.add(meng, lambda e: e.tensor_tensor(
                            out=pm[bi][:], in0=pex[bi][:], in1=EB4[:, 8 * g + 2 * hp:8 * g + 2 * hp + 2, :, :], op=ALU.mult),
                            reads=[("pex", bi, 0), ("pex", bi, 1), "EB"], writes=[("pm", bi, 0), ("pm", bi, 1)])

                    def stageO(r, b, bi, oi, st_=st_, d=d, nti=nti, g=g):
                        q0 = r + 128 * b * d
                        qsl = slice(q0, q0 + 127 * d + 1, d) if d > 1 else slice(q0, q0 + 128)
                        n_ = 0
                        for hh in range(2):
                            for i in range(2):
                                tn = r * nti + b + i
                                P.add("pe", lambda e, hh=hh, i=i, tn=tn, n_=n_: e.matmul(
                                    pO[oi][:, :], lhsT=Vb[:, tn, hh, :], rhs=pm[bi][:, hh, i, :],
                                    start=(n_ == 0), stop=(n_ == 3)),
                                    reads=[("Vb", hh, tn), ("pm", bi, i)], writes=[("pO", oi)])
                                n_ += 1
                        n_ = 0
                        for hh in range(2):
                            for i in range(2):
                                P.add("pe", lambda e, hh=hh, i=i, n_=n_: e.matmul(
                                    pD[oi][:, :], lhsT=ones2[:, hh, :], rhs=pm[bi][:, hh, i, :],
                                    start=(n_ == 0), stop=(n_ == 3)),
                                    reads=["ones2", ("pm", bi, i)], writes=[("pD", oi)])
                                n_ += 1
                        P.add("dve", lambda e: e.tensor_tensor(
                            out=U[g][:, qsl], in0=pO[oi][:, :], in1=ZT[st_][:, qsl], op=ALU.mult),
                            reads=[("pO", oi), ("ZT", st_)], writes=[("U", g)])
                        if g == 0:
                            P.add("act", lambda e: e.activation(out=Dn[:, qsl], in_=pD[oi][:, :], func=AF.Copy),
                                  reads=[("pD", oi)], writes=["Dn"])
                        else:
                            P.add("dve", lambda e: e.tensor_tensor(
                                out=Dn[:, qsl], in0=pD[oi][:, :], in1=Dn[:, qsl], op=ALU.add),
                                reads=[("pD", oi), "Dn"], writes=["Dn"])

                    blocks = [(r, b) for r in range(d) for b in range(nb)]
                    inflight = []
                    for blk in blocks + [None, None, None]:
                        if blk is not None:
                            bi = bc % 4
                            oi = bc % 2
                            bc += 1
                            stageS(blk[0], blk[1], bi)
                            inflight.append((blk[0], blk[1], bi, oi))
                        if len(inflight) > 3 or (blk is None and inflight):
                            stageO(*inflight.pop(0))
                    if g == 2:
                        P.add("act", lambda e: e.activation(out=Dn[:, :], in_=Dn[:, :], func=AF.Ln), reads=["Dn"], writes=["Dn"])
                        P.add("act", lambda e: e.activation(out=Dn[:, :], in_=Dn[:, :], func=AF.Exp, scale=-1.0),
                              reads=["Dn"], writes=["Dn"])
                        for g2 in range(3):
                            yi = yc % 2
                            yc += 1
                            c2 = g2 * 4 + hp
                            P.add("dve" if g2 != 1 else "pool", lambda e, g2=g2, yi=yi: e.tensor_tensor(
                                out=yb[yi][:, :], in0=U[g2][:, :], in1=Dn[:, :], op=ALU.mult),
                                reads=[("U", g2), "Dn"], writes=[("yb", yi)])
                            P.add("pool", lambda e, yi=yi, c2=c2: e.dma_start(out=YS[c2, :, :], in_=yb[yi][:, :]),
                                  reads=[("yb", yi)], writes=[("YS", c2)], dma=("yb", yi))
            P.flush()

        if debug == "C":
            return nc

        with ExitStack() as st:
            def sb(name, shape, dt):
                return st.enter_context(nc.sbuf_tensor(uniq(name), list(shape), dt))

            def psum(name, shape, dt):
                return st.enter_context(nc.psum_tensor(uniq(name), list(shape), dt))

            WO1 = sb("WO1", [128, 12, DM], BF16)
            wst = [sb("wstd%d" % i, [128, DM], F32) for i in range(4)]
            yT = [sb("yT%d" % i, [128, 12, 512], BF16) for i in range(2)]
            h1s = [sb("h1s%d" % i, [128, DM], F32) for i in range(3)]
            po = [psum("pod%d" % i, [128, 512], F32) for i in range(3)]
            for c in range(12):
                P.add("sp", lambda e, c=c: e.dma_start(out=WO1[:, c, :], in_=WO1S[c]),
                      writes=[("WO1", c)], dma=("wo1d", c % 4))
            hc = 0
            pc = 0
            for tt in range(8):
                yi = tt % 2
                P.add("sp", lambda e, yi=yi, tt=tt: e.dma_start(
                    out=yT[yi][:, :, :], in_=YS[:, :, 512 * tt:512 * tt + 512].rearrange("c p t -> p c t")),
                    writes=[("yT", yi)], dma=("yT", yi))
                for s in range(4):
                    hi = hc % 3
                    hc += 1
                    r0 = HALO + 512 * tt + 128 * s
                    P.add("sp", lambda e, hi=hi, r0=r0: e.dma_start(out=h1s[hi][:, :], in_=H1[r0 + 1:r0 + 129, :]),
                          writes=[("h1s", hi)], dma=("h1s", hi))
                    for half in range(2):
                        bi = pc % 3
                        pc += 1
                        for c in range(12):
                            P.add("pe", lambda e, bi=bi, c=c, yi=yi, s=s, half=half: e.matmul(
                                po[bi][:, :], lhsT=yT[yi][:, c, 128 * s:128 * s + 128],
                                rhs=WO1[:, c, half * 512:(half + 1) * 512], start=(c == 0), stop=(c == 11)),
                                reads=[("yT", yi), ("WO1", c)], writes=[("po", bi)])
                        P.add("dve", lambda e, bi=bi, hi=hi, half=half: e.tensor_tensor(
                            out=h1s[hi][:, half * 512:(half + 1) * 512], in0=po[bi][:, :],
                            in1=h1s[hi][:, half * 512:(half + 1) * 512], op=ALU.add),
                            reads=[("po", bi), ("h1s", hi)], writes=[("h1s", hi)])
                    o0 = 512 * tt + 128 * s
                    P.add("pool", lambda e, hi=hi, o0=o0: e.dma_start(out=outp[o0:o0 + 128, :], in_=h1s[hi][:, :]),
                          reads=[("h1s", hi)], dma=("outw", hi))
            P.flush()
    return nc


def make_in_maps(x, norm_g, conv_w_in, conv_kernel, conv_bias, conv_w_out, attn_w_in,
                 q_norm_g, k_norm_g, attn_w_out, rel_bias_table):
    consts = host_consts()
    f32 = np.float32
    shared = {
        "norm_g": np.ascontiguousarray(norm_g, f32),
        "conv_w_in": np.ascontiguousarray(conv_w_in[0], f32),
        "cw": np.ascontiguousarray(conv_kernel[0].reshape(3, 16, 128).transpose(2, 1, 0), f32),
        "cb": np.ascontiguousarray(conv_bias[0].reshape(16, 128).T, f32),
        "conv_w_out": np.ascontiguousarray(conv_w_out[0], f32),
        "attn_w_in": np.ascontiguousarray(attn_w_in[0], f32),
        "gqT": np.ascontiguousarray(q_norm_g[0].reshape(12, 128).T, f32),
        "gkT": np.ascontiguousarray(k_norm_g[0].reshape(12, 128).T, f32),
        "attn_w_out": np.ascontiguousarray(attn_w_out[0], f32),
        "rel_bias_table": np.ascontiguousarray(rel_bias_table, f32),
    }
    shared.update(consts)
    in_maps = []
    for c in range(NCORES):
        b, q = divmod(c, 4)
        own0 = q * OWN
        xs = np.zeros((XS_ROWS, DM), f32)
        p0 = own0 - HALO - 1
        lo, hi = max(0, p0), min(S_LEN, p0 + XS_ROWS)
        xs[lo - p0:hi - p0] = x[b, lo:hi]
        fl = np.zeros((128, 3, 48), f32)
        for g, d in enumerate(DIL):
            nti = 32 // d + 1
            reg0 = HALO - 64 * d
            t = np.arange(128)[:, None, None]
            r = np.arange(d)[None, :, None]
            i = np.arange(nti)[None, None, :]
            j = reg0 + r + d * (128 * i + t)
            p = own0 - HALO + j
            fl[:, g, :d * nti] = ((p >= 0) & (p < S_LEN)).reshape(128, d * nti).astype(f32)
        m = dict(shared)
        m["xs"] = xs
        m["fl"] = fl
        in_maps.append(m)
    return in_maps


def kernel(x, norm_g, conv_w_in, conv_kernel, conv_bias, conv_w_out, attn_w_in,
           q_norm_g, k_norm_g, attn_w_out, rel_bias_table):
    args = [np.asarray(a) for a in (x, norm_g, conv_w_in, conv_kernel, conv_bias, conv_w_out, attn_w_in,
                                    q_norm_g, k_norm_g, attn_w_out, rel_bias_table)]
    in_maps = make_in_maps(*args)
    nc = build()
    res = run_bass_kernel_spmd(nc, in_maps, core_ids=list(range(NCORES)))
    out = np.empty((2, S_LEN, DM), np.float32)
    for c in range(NCORES):
        b, q = divmod(c, 4)
        out[b, q * OWN:(q + 1) * OWN] = res.results[c]["out"]
    return out
```

```python
import numpy as np
import ml_dtypes
from contextlib import ExitStack
import concourse.bass as bass
import concourse.mybir as mybir
from concourse.bass_utils import run_bass_kernel_spmd

F32 = mybir.dt.float32
BF16 = mybir.dt.bfloat16
ALU = mybir.AluOpType
AF = mybir.ActivationFunctionType

S_LEN = 16384
DM = 1024
NCORES = 8
DIRECT_CAST = True
OWN = 4096
HALO = 1024
REG = OWN + 2 * HALO
TA = 473
WA = 475
NTA = 13
XS_ROWS = TA * (NTA - 1) + WA
H1_ROWS = XS_ROWS + 1
EPS = 1e-6
DIL = (1, 4, 16)
VLEN = 384
ENGS = ("pe", "act", "dve", "pool", "sp")


class Op:
    __slots__ = ("eng", "fn", "reads", "writes", "dma", "idx", "waits", "sig", "val")

    def __init__(self, eng, fn, reads, writes, dma):
        self.eng = eng
        self.fn = fn
        self.reads = tuple(reads)
        self.writes = tuple(writes)
        self.dma = dma
        self.waits = []
        self.sig = None
        self.val = 0


class Prog:
    def __init__(self, nc, stack):
        self.nc = nc
        self.stack = stack
        self.ops = []
        self.sems = {}
        self.counts = {}
        self.waited = {e: {} for e in ENGS}

    def add(self, eng, fn, reads=(), writes=(), dma=None):
        writes = list(writes)
        if dma is not None:
            writes.append(("__dmakey", dma))
        self.ops.append(Op(eng, fn, reads, writes, dma))

    def _sem(self, key):
        if key not in self.sems:
            self.sems[key] = self.stack.enter_context(self.nc.semaphore("s%d" % len(self.sems)))
        return self.sems[key]

    def flush(self):
        ops = self.ops
        self.ops = []
        if not ops:
            return
        last_w, readers, deps_of = {}, {}, []
        need = set()
        for i, op in enumerate(ops):
            deps = set()
            for b in op.reads:
                if b in last_w:
                    deps.add(last_w[b])
            for b in op.writes:
                if b in last_w:
                    deps.add(last_w[b])
                deps.update(readers.get(b, ()))
            deps.discard(i)
            d2 = set()
            for d in deps:
                p = ops[d]
                if p.eng == "pe" and op.eng == "pe" and p.dma is None and op.dma is None:
                    continue
                d2.add(d)
            deps_of.append(d2)
            need |= d2
            for b in op.reads:
                readers.setdefault(b, []).append(i)
            for b in op.writes:
                last_w[b] = i
                readers[b] = []
        last_of = {}
        for i, op in enumerate(ops):
            if op.dma is not None:
                need.add(i)
            else:
                last_of[op.eng] = i
        need |= set(last_of.values())
        for i, op in enumerate(ops):
            if i not in need:
                continue
            key = ("dma", op.dma) if op.dma is not None else ("eng", op.eng)
            self.counts[key] = self.counts.get(key, 0) + (16 if op.dma is not None else 1)
            op.sig, op.val = key, self.counts[key]
            self._sem(key)
        for i, op in enumerate(ops):
            w = {}
            for d in deps_of[i]:
                p = ops[d]
                if w.get(p.sig, 0) < p.val:
                    w[p.sig] = p.val
            for k, v in w.items():
                if self.waited[op.eng].get(k, 0) >= v:
                    continue
                self.waited[op.eng][k] = v
                op.waits.append((k, v))
        per = {e: [op for op in ops if op.eng == e] for e in ENGS}
        sems, counts, waited = self.sems, dict(self.counts), self.waited

        def run(name, eng):
            for op in per[name]:
                for k, v in op.waits:
                    eng.wait_ge(sems[k], v)
                ins = op.fn(eng)
                if op.sig is not None:
                    ins.then_inc(sems[op.sig], 16 if op.dma is not None else 1)
            for k, v in counts.items():
                if waited[name].get(k, 0) < v:
                    eng.wait_ge(sems[k], v)
                    waited[name][k] = v

        with self.nc.Block() as block:
            @block.sync
            def _(e):
                run("sp", e)

            @block.tensor
            def _(e):
                run("pe", e)

            @block.scalar
            def _(e):
                run("act", e)

            @block.vector
            def _(e):
                run("dve", e)

            @block.gpsimd
            def _(e):
                run("pool", e)


def t5_bucket(rel):
    nb = 16
    ret = (rel > 0).astype(np.int32) * nb
    n = np.abs(rel)
    max_exact = nb // 2
    large = max_exact + (np.log(np.maximum(n, 1) / max_exact)
                         / np.log(1024 / max_exact) * (nb - max_exact)).astype(np.int32)
    large = np.minimum(large, nb - 1)
    return ret + np.where(n < max_exact, n, large).astype(np.int32)


def host_consts():
    bf = ml_dtypes.bfloat16
    c = {}
    c["ident"] = np.eye(128, dtype=np.float32).astype(bf)
    bo = np.zeros((128, 128), np.float32)
    bo[:64, :64] = 1.0 / 64
    bo[64:, 64:] = 1.0 / 64
    c["bones"] = bo.astype(bf)
    c["jflip"] = np.eye(128, dtype=np.float32)[::-1].copy().astype(bf)
    on = np.zeros((128, 2, 128), np.float32)
    on[:, 0, :64] = 1.0
    on[:, 1, 64:] = 1.0
    c["ones2"] = on.astype(bf)
    oh = np.zeros((32, 3, VLEN), np.float32)
    win = np.zeros((8, VLEN), np.float32)
    m = np.arange(383)
    delta = 191 - m
    for g, d in enumerate(DIL):
        bk = t5_bucket(delta * d)
        oh[bk, g, m] = 1.0
    win[:, :383] = (np.abs(delta) <= 64).astype(np.float32)[None, :]
    c["oh"] = oh
    c["win"] = win
    hm = np.zeros((128, 2), np.float32)
    hm[:64, 0] = 1.0
    hm[64:, 1] = 1.0
    c["hmask"] = hm
    return c


def build(debug=None):
    nc = bass.Bass("TRN2", target_bir_lowering=False)
    _uc = [0]

    def uniq(name):
        _uc[0] += 1
        return "t%d_%s" % (_uc[0], name)

    def din(name, shape, dt=F32):
        return nc.dram_tensor(name, list(shape), dt, kind="ExternalInput")

    xs_h = din("xs", [XS_ROWS, DM])
    ng_h = din("norm_g", [2, DM])
    w0_h = din("conv_w_in", [DM, 8192])
    cw_h = din("cw", [128, 16, 3])
    cb_h = din("cb", [128, 16])
    wo0_h = din("conv_w_out", [2048, DM])
    w1_h = din("attn_w_in", [DM, 6144])
    gq_h = din("gqT", [128, 12])
    gk_h = din("gkT", [128, 12])
    wo1_h = din("attn_w_out", [1536, DM])
    tb_h = din("rel_bias_table", [32, 24])
    fl_h = din("fl", [128, 3, 48])
    ident_h = din("ident", [128, 128], BF16)
    bones_h = din("bones", [128, 128], BF16)
    jflip_h = din("jflip", [128, 128], BF16)
    ones2_h = din("ones2", [128, 2, 128], BF16)
    oh_h = din("oh", [32, 3, VLEN])
    win_h = din("win", [8, VLEN])
    hm_h = din("hmask", [128, 2])
    out_h = nc.dram_tensor("out", [OWN, DM], F32, kind="ExternalOutput")

    H1_h = nc.dram_tensor("H1", [H1_ROWS, DM], F32, kind="ExternalOutput") if debug == "A" else nc.dram_tensor("H1", [H1_ROWS, DM], F32)
    H1a_h = nc.dram_tensor("H1a", [H1_ROWS, DM], F32)
    QS_h = nc.dram_tensor("QS", [12, 2, 128, OWN], BF16)
    KS_h = nc.dram_tensor("KS", [12, 128, REG], BF16)
    ZS_h = nc.dram_tensor("ZS", [12, 128, OWN], BF16)
    VS_h = nc.dram_tensor("VS", [12, REG, 128], BF16)
    YS_h = nc.dram_tensor("YS", [12, 128, OWN], BF16)
    VEC_h = nc.dram_tensor("VEC", [24, VLEN], BF16)
    W0S_h = nc.dram_tensor("W0S", [8, 4, 128, 8, 128], BF16)
    WOS_h = nc.dram_tensor("WOS", [8, 128, DM], BF16)
    W1S_h = nc.dram_tensor("W1S", [48, 128, 8, 128], BF16)
    WO1S_h = nc.dram_tensor("WO1S", [12, 128, DM], BF16)

    xs, ng, w0, cw, cb, wo0, w1 = xs_h.ap(), ng_h.ap(), w0_h.ap(), cw_h.ap(), cb_h.ap(), wo0_h.ap(), w1_h.ap()
    wo1, outp, H1 = wo1_h.ap(), out_h.ap(), H1_h.ap()
    H1a = H1a_h.ap()
    QS, KS, ZS, VS, YS, VEC = QS_h.ap(), KS_h.ap(), ZS_h.ap(), VS_h.ap(), YS_h.ap(), VEC_h.ap()
    W0S, WOS, W1S, WO1S = W0S_h.ap(), WOS_h.ap(), W1S_h.ap(), WO1S_h.ap()

    with ExitStack() as gst:
        P = Prog(nc, gst)

        def gsb(name, shape, dt):
            return gst.enter_context(nc.sbuf_tensor(uniq(name), list(shape), dt))

        ident = gsb("ident", [128, 128], BF16)
        bones = gsb("bones", [128, 128], BF16)
        jflip = gsb("jflip", [128, 128], BF16)
        ones2 = gsb("ones2", [128, 2, 128], BF16)
        EB = gsb("EB", [128, 24 * 2 * 128], BF16)
        FL = gsb("FL", [128, 3, 48], F32)
        NB = gsb("NB", [128, 3, 48], F32)
        gqA = gsb("gqA", [128, 12], F32)
        gqB = gsb("gqB", [128, 12], F32)
        gkS = gsb("gkS", [128, 12], F32)
        mh = gsb("mh", [128, 1], F32)
        cwt = gsb("cwt", [128, 16, 3], F32)
        cbt = gsb("cbt", [128, 16], F32)
        EB4 = EB[:, :].rearrange("p (h i q) -> p h i q", h=24, i=2)

        with ExitStack() as st:
            def sb(name, shape, dt):
                return st.enter_context(nc.sbuf_tensor(uniq(name), list(shape), dt))

            def psum(name, shape, dt):
                return st.enter_context(nc.psum_tensor(uniq(name), list(shape), dt))

            tb = sb("tb", [32, 24], F32)
            oh = sb("oh", [32, 3, VLEN], F32)
            win = sb("win", [8, VLEN], F32)
            hm = sb("hm", [128, 2], F32)
            gtmp = sb("gtmp", [128, 12], F32)
            gtmp2 = sb("gtmp2", [128, 12], F32)
            ev = sb("ev", [8, 3, VLEN], F32)
            evb = sb("evb", [8, 3, VLEN], BF16)
            EBp = sb("EBp", [128, 24 * 2 * 128], BF16)
            EBp4 = EBp[:, :].rearrange("p (h i q) -> p h i q", h=24, i=2)
            pV = psum("pV", [128, 512], F32)
            pE = [psum("pE%d" % i, [128, 512], F32) for i in range(2)]

            for nm, t_, h_ in (("ident", ident, ident_h), ("bones", bones, bones_h), ("jflip", jflip, jflip_h),
                               ("ones2", ones2, ones2_h), ("FL", FL, fl_h), ("tb", tb, tb_h), ("oh", oh, oh_h),
                               ("win", win, win_h), ("hm", hm, hm_h), ("cwt", cwt, cw_h), ("cbt", cbt, cb_h),
                               ("gtmp", gtmp, gq_h), ("gtmp2", gtmp2, gk_h)):
                P.add("sp", lambda e, t_=t_, h_=h_: e.dma_start(out=t_[:], in_=h_.ap()), writes=[nm], dma="const")
            P.add("pool", lambda e: e.memset(mh[:], -0.5), writes=["mh"])
            P.add("dve", lambda e: e.tensor_scalar(out=NB[:], in0=FL[:], scalar1=30000.0, scalar2=-30000.0,
                                                   op0=ALU.mult, op1=ALU.add), reads=["FL"], writes=["NB"])
            P.add("dve", lambda e: e.tensor_scalar(out=gqA[:], in0=gtmp[:], scalar1=0.125, scalar2=hm[:, 0:1],
                                                   op0=ALU.mult, op1=ALU.mult), reads=["gtmp", "hm"], writes=["gqA"])
            P.add("dve", lambda e: e.tensor_scalar(out=gqB[:], in0=gtmp[:], scalar1=0.125, scalar2=hm[:, 1:2],
                                                   op0=ALU.mult, op1=ALU.mult), reads=["gtmp", "hm"], writes=["gqB"])
            P.add("dve", lambda e: e.tensor_copy(out=gkS[:], in_=gtmp2[:]), reads=["gtmp2"], writes=["gkS"])
            for g in range(3):
                P.add("pe", lambda e, g=g: e.matmul(pV[0:8, 0:VLEN], lhsT=tb[:, 8 * g:8 * g + 8], rhs=oh[:, g, :],
                                                    start=True, stop=True), reads=["tb", "oh"], writes=["pV"])
                P.add("act", lambda e, g=g: e.activation(out=ev[:, g, :], in_=pV[0:8, 0:VLEN], func=AF.Exp),
                      reads=["pV"], writes=[("ev", g)])
                P.add("dve", lambda e, g=g: e.tensor_tensor(out=evb[:, g, :], in0=ev[:, g, :], in1=win[:], op=ALU.mult),
                      reads=[("ev", g), "win"], writes=[("evb", g)])
                P.add("sp", lambda e, g=g: e.dma_start(out=VEC[8 * g:8 * g + 8, :], in_=evb[:, g, :]),
                      reads=[("evb", g)], writes=["VEC"], dma="const")
            for i in range(2):
                src = bass.AP(VEC_h, 128 - 128 * i, [[1, 128], [VLEN, 24], [1, 128]])
                P.add("sp", lambda e, i=i, src=src: e.dma_start(out=EBp4[:, :, i, :], in_=src),
                      reads=["VEC"], writes=["EBp"], dma="const")
            for n in range(12):
                pe_ = pE[n % 2]
                P.add("pe", lambda e, n=n, pe_=pe_: e.matmul(pe_[:, :], lhsT=jflip[:], rhs=EBp[:, 512 * n:512 * n + 512],
                                                             start=True, stop=True),
                      reads=["jflip", "EBp"], writes=[("pE", n % 2)])
                P.add("act" if n % 2 else "dve",
                      (lambda e, n=n, pe_=pe_: e.activation(out=EB[:, 512 * n:512 * n + 512], in_=pe_[:, :], func=AF.Copy)) if n % 2
                      else (lambda e, n=n, pe_=pe_: e.tensor_copy(out=EB[:, 512 * n:512 * n + 512], in_=pe_[:, :])),
                      reads=[("pE", n % 2)], writes=[("EB", n)])
            P.flush()

        MS = (128, 128, 128, WA - 384)
        for pa in range(2):
            with ExitStack() as st:
                def sb(name, shape, dt):
                    return st.enter_context(nc.sbuf_tensor(uniq(name), list(shape), dt))

                def psum(name, shape, dt):
                    return st.enter_context(nc.psum_tensor(uniq(name), list(shape), dt))

                NWS = 10
                W0 = sb("W0", [128, 8, 4, 8, 128], BF16)
                WO = sb("WO", [128, 8, DM], BF16)
                wst = [sb("wst%d" % i, [128, 512], F32) for i in range(NWS)]
                gb = sb("gb", [128, DM], F32)
                xa = [sb("xa%d" % i, [128, DM], F32) for i in range(4)]
                xr = [sb("xr%d" % i, [128, DM], F32) for i in range(4)]
                xn = [sb("xn%d" % i, [128, DM], BF16) for i in range(4)]
                ss = [sb("ss%d" % i, [128, 1], F32) for i in range(4)]
                vv = [sb("vv%d" % i, [128, 1], F32) for i in range(4)]
                rs = [sb("rs%d" % i, [128, 1], F32) for i in range(4)]
                xnT = [sb("xnT%d" % i, [128, 8, WA], BF16) for i in range(2)]
                ybuf = sb("ybuf", [128, 8, WA], BF16)
                usb = [sb("usb%d" % i, [128, WA], F32) for i in range(2)]
                cgu = [sb("cgu%d" % i, [128, WA], F32) for i in range(2)]
                szb = [sb("szb%d" % i, [128, WA], BF16) for i in range(2)]
                gat = [sb("gat%d" % i, [128, WA], F32) for i in range(2)]
                tcv = [sb("tcv%d" % i, [128, WA], F32) for i in range(2)]
                NPJ = 4
                pj = [psum("pj%d" % i, [128, 512], F32) for i in range(NPJ)]
                pT = [psum("pT%d" % i, [128, 8, 128], BF16) for i in range(2)]
                po = [psum("po%d" % i, [128, 512], F32) for i in range(2)]

                P.add("sp", lambda e: e.dma_start(out=gb[:], in_=ng[0:1, :].partition_broadcast(128)),
                      writes=["gb"], dma="gb")
                P.add("pool", lambda e: e.memset(ybuf[:], 0.0), writes=["ybuf"])
                wcnt = [0]

                def load_w0(el):
                    e_ = 8 * pa + el
                    if pa == 1:
                        for part in (2, 1, 3, 0):
                            P.add("sp", lambda e, part=part, el=el: e.dma_start(out=W0[:, :, part, el, :], in_=W0S[el, part]),
                                  writes=[("W0", part, el, 0), ("W0", part, el, 1)], dma=("w0d", (4 * el + part) % 4))
                        return
                    if DIRECT_CAST:
                        for part in (2, 1, 3, 0):
                            c0 = part * 2048 + e_ * 128
                            P.add("pool", lambda e, part=part, el=el, c0=c0: e.dma_start(
                                out=W0[:, :, part, el, :], in_=w0[:, c0:c0 + 128].rearrange("(k p) c -> p k c", p=128)),
                                writes=[("W0", part, el, 0), ("W0", part, el, 1)], dma=("w0d", (4 * el + part) % 4))
                        return
                    for part in (2, 1, 3, 0):
                        c0 = part * 2048 + e_ * 128
                        for kh in range(2):
                            sl = wcnt[0] % NWS
                            wcnt[0] += 1
                            P.add("sp", lambda e, sl=sl, c0=c0, kh=kh: e.dma_start(
                                out=wst[sl][:, :].rearrange("p (k c) -> p k c", k=4),
                                in_=w0[kh * 512:(kh + 1) * 512, c0:c0 + 128].rearrange("(k p) c -> p k c", p=128)),
                                writes=[("wst", sl)], dma=("wst", sl))
                            ceng = ("act", "dve", "act", "dve", "pool")[wcnt[0] % 5]
                            if ceng == "act":
                                P.add("act", lambda e, sl=sl, part=part, el=el, kh=kh: e.activation(
                                    out=W0[:, 4 * kh:4 * kh + 4, part, el, :],
                                    in_=wst[sl][:, :].rearrange("p (k c) -> p k c", k=4), func=AF.Copy),
                                    reads=[("wst", sl)], writes=[("W0", part, el, kh)])
                            else:
                                P.add(ceng, lambda e, sl=sl, part=part, el=el, kh=kh: e.tensor_copy(
                                    out=W0[:, 4 * kh:4 * kh + 4, part, el, :],
                                    in_=wst[sl][:, :].rearrange("p (k c) -> p k c", k=4)),
                                    reads=[("wst", sl)], writes=[("W0", part, el, kh)])

                def load_wo(el):
                    e_ = 8 * pa + el
                    if pa == 1:
                        P.add("sp", lambda e, el=el: e.dma_start(out=WO[:, el, :], in_=WOS[el]),
                              writes=[("WO", el, 0), ("WO", el, 1)], dma=("wod", el % 4))
                        return
                    if DIRECT_CAST:
                        P.add("pool", lambda e, el=el, e_=e_: e.dma_start(out=WO[:, el, :], in_=wo0[e_ * 128:(e_ + 1) * 128, :]),
                              writes=[("WO", el, 0), ("WO", el, 1)], dma=("wod", el % 4))
                        return
                    for hf in range(2):
                        sl = wcnt[0] % NWS
                        wcnt[0] += 1
                        P.add("sp", lambda e, sl=sl, e_=e_, hf=hf: e.dma_start(
                            out=wst[sl][:, :], in_=wo0[e_ * 128:(e_ + 1) * 128, hf * 512:(hf + 1) * 512]),
                            writes=[("wst", sl)], dma=("wst", sl))
                        P.add("pool", lambda e, sl=sl, el=el, hf=hf: e.tensor_copy(out=WO[:, el, hf * 512:(hf + 1) * 512], in_=wst[sl][:, :]),
                              reads=[("wst", sl)], writes=[("WO", el, hf)])

                bg = []
                if pa == 0:
                    for el in range(8):
                        for part in (2, 1, 3, 0):
                            c0 = part * 2048 + (8 + el) * 128
                            bg.append((W0S[el, part], w0[:, c0:c0 + 128].rearrange("(k p) c -> p k c", p=128)))
                        bg.append((WOS[el], wo0[(8 + el) * 128:(9 + el) * 128, :]))
                    for n_ in range(48):
                        bg.append((W1S[n_], w1[:, 128 * n_:128 * n_ + 128].rearrange("(k p) c -> p k c", p=128)))
                    for c_ in range(12):
                        bg.append((WO1S[c_], wo1[c_ * 128:(c_ + 1) * 128, :]))
                bgc = [0]

                def bg_cast(n):
                    for _ in range(n):
                        if bg:
                            dst, src = bg.pop(0)
                            P.add("pool", lambda e, dst=dst, src=src: e.dma_start(out=dst, in_=src),
                                  dma=("bg", bgc[0] % 4))
                            bgc[0] += 1

                scnt = [0]
                pjc = [0]
                poc = [0]

                def prep_dma(t):
                    for s in range(4):
                        M = MS[s]
                        r0 = TA * t + 128 * s
                        P.add("sp", lambda e, s=s, r0=r0, M=M: e.dma_start(out=xa[s][0:M, :], in_=xs[r0:r0 + M, :]),
                              writes=[("xa", s)], dma=("xa", s))

                def prep_sub(t, s):
                    M = MS[s]
                    sl = s
                    P.add("act", lambda e, sl=sl, s=s, M=M: e.activation(out=xn[s][0:M, :], in_=xa[sl][0:M, :],
                                                                        func=AF.Square, accum_out=ss[sl][0:M, :]),
                          reads=[("xa", sl)], writes=[("xn", s), ("ss", sl)])
                    P.add("dve", lambda e, sl=sl, M=M: e.tensor_scalar(out=vv[sl][0:M, :], in0=ss[sl][0:M, :],
                                                                      scalar1=1.0 / DM, scalar2=EPS,
                                                                      op0=ALU.mult, op1=ALU.add),
                          reads=[("ss", sl)], writes=[("vv", sl)])
                    P.add("pool", lambda e, sl=sl, M=M: e.tensor_tensor(out=rs[sl][0:M, :], in0=vv[sl][0:M, :],
                                                                       in1=mh[0:M, :], op=ALU.pow),
                          reads=[("vv", sl), "mh"], writes=[("rs", sl)])
                    P.add("dve", lambda e, sl=sl, s=s, M=M: e.scalar_tensor_tensor(
                        out=xn[s][0:M, :], in0=xa[sl][0:M, :], scalar=rs[sl][0:M, :], in1=gb[0:M, :],
                        op0=ALU.mult, op1=ALU.mult),
                        reads=[("xa", sl), ("rs", sl), "gb"], writes=[("xn", s)])

                def prep_load(t):
                    prep_dma(t)
                    for s in range(4):
                        prep_sub(t, s)

                def prep_T(t, subs=(0, 1, 2, 3)):
                    xi = t % 2
                    for s in subs:
                        M = MS[s]
                        pi = s % 2
                        for k in range(8):
                            P.add("pe", lambda e, s=s, k=k, M=M, pi=pi: e.transpose(out=pT[pi][:, k, 0:M],
                                                                                    in_=xn[s][0:M, k * 128:(k + 1) * 128],
                                                                                    identity=ident[0:M, 0:M]),
                                  reads=[("xn", s), "ident"], writes=[("pT", pi)])
                        if s % 2:
                            P.add("act", lambda e, s=s, M=M, pi=pi, xi=xi: e.activation(
                                out=xnT[xi][:, :, 128 * s:128 * s + M], in_=pT[pi][:, :, 0:M], func=AF.Copy),
                                reads=[("pT", pi)], writes=[("xnT", xi)])
                        else:
                            P.add("dve", lambda e, s=s, M=M, pi=pi, xi=xi: e.tensor_copy(
                                out=xnT[xi][:, :, 128 * s:128 * s + M], in_=pT[pi][:, :, 0:M]),
                                reads=[("pT", pi)], writes=[("xnT", xi)])

                def phase1(t, nxt=None):
                    xi = t % 2
                    for el in range(8):
                        if t == 0 and pa == 0 and DIRECT_CAST:
                            if el + 2 < 8:
                                load_w0(el + 2)
                        elif t == 0:
                            load_w0(el)
                        if nxt is not None and el % 2 == 1:
                            prep_sub(nxt, el // 2)
                        e_ = 8 * pa + el
                        st_ = el % 2
                        bank = {}
                        for part in (2, 1, 3, 0):
                            bi = pjc[0] % NPJ
                            pjc[0] += 1
                            bank[part] = bi
                            for k in range(8):
                                P.add("pe", lambda e, bi=bi, k=k, part=part, el=el, xi=xi: e.matmul(
                                    pj[bi][:, 0:WA], lhsT=W0[:, k, part, el, :], rhs=xnT[xi][:, k, :],
                                    start=(k == 0), stop=(k == 7)),
                                    reads=[("W0", part, el, k // 4), ("xnT", xi)], writes=[("pj", bi)])
                        bB, bC, bU, bZ = bank[0], bank[1], bank[2], bank[3]
                        P.add("act", lambda e, st_=st_, bU=bU: e.activation(out=usb[st_][:, :], in_=pj[bU][:, 0:WA], func=AF.Copy),
                              reads=[("pj", bU)], writes=[("usb", st_)])
                        P.add("dve", lambda e, st_=st_, bC=bC: e.tensor_tensor(out=cgu[st_][:, :], in0=pj[bC][:, 0:WA],
                                                                             in1=usb[st_][:, :], op=ALU.mult),
                              reads=[("pj", bC), ("usb", st_)], writes=[("cgu", st_)])
                        P.add("act", lambda e, st_=st_, e_=e_: e.activation(
                            out=tcv[st_][:, 1:WA - 1], in_=cgu[st_][:, 1:WA - 1], func=AF.Identity,
                            scale=cwt[:, e_, 1:2], bias=cbt[:, e_:e_ + 1]),
                            reads=[("cgu", st_), "cwt", "cbt"], writes=[("tcv", st_)])
                        P.add("act", lambda e, st_=st_, bZ=bZ: e.activation(out=szb[st_][:, :], in_=pj[bZ][:, 0:WA], func=AF.Silu),
                              reads=[("pj", bZ)], writes=[("szb", st_)])
                        P.add("dve", lambda e, st_=st_, bB=bB: e.tensor_tensor(out=gat[st_][:, :], in0=pj[bB][:, 0:WA],
                                                                             in1=szb[st_][:, :], op=ALU.mult),
                              reads=[("pj", bB), ("szb", st_)], writes=[("gat", st_)])
                        P.add("dve", lambda e, st_=st_, e_=e_: e.scalar_tensor_tensor(
                            out=tcv[st_][:, 1:WA - 1], in0=cgu[st_][:, 0:WA - 2], scalar=cwt[:, e_, 0:1],
                            in1=tcv[st_][:, 1:WA - 1], op0=ALU.mult, op1=ALU.add),
                            reads=[("cgu", st_), ("tcv", st_), "cwt"], writes=[("tcv", st_)])
                        P.add("dve", lambda e, st_=st_, e_=e_: e.scalar_tensor_tensor(
                            out=tcv[st_][:, 1:WA - 1], in0=cgu[st_][:, 2:WA], scalar=cwt[:, e_, 2:3],
                            in1=tcv[st_][:, 1:WA - 1], op0=ALU.mult, op1=ALU.add),
                            reads=[("cgu", st_), ("tcv", st_), "cwt"], writes=[("tcv", st_)])
                        P.add("pool", lambda e, st_=st_, el=el: e.tensor_tensor(
                            out=ybuf[:, el, 1:WA - 1], in0=tcv[st_][:, 1:WA - 1], in1=gat[st_][:, 1:WA - 1], op=ALU.mult),
                            reads=[("tcv", st_), ("gat", st_)], writes=[("ybuf", el)])
                        if t == 0:
                            load_wo(el)
                        elif t >= 2:
                            bg_cast(2)
                        if nxt is not None and el in (2, 4, 6):
                            prep_T(nxt, (el // 2 - 1,))

                def xr_dma(t):
                    for s in range(4):
                        M = MS[s]
                        plo = 1 if s == 0 else 0
                        phi = min(M, TA + 1 - 128 * s)
                        if t == 0 and s == 0:
                            plo = 0
                        if t == NTA - 1 and s == 3:
                            phi = M
                        jlo = TA * t + 128 * s + plo - 1
                        n = phi - plo
                        if pa == 0:
                            r0 = TA * t + 128 * s
                            P.add("sp", lambda e, s=s, r0=r0, M=M: e.dma_start(out=xr[s][0:M, :], in_=xs[r0:r0 + M, :]),
                                  writes=[("xr", s)], dma=("xr", s))
                        else:
                            r0 = TA * t + 128 * s
                            P.add("sp", lambda e, s=s, r0=r0, M=M: e.dma_start(out=xr[s][0:M, :], in_=H1a[r0:r0 + M, :]),
                                  writes=[("xr", s)], dma=("xr", s))

                def phase2(t):
                    for s in range(4):
                        M = MS[s]
                        sl = s
                        plo = 1 if s == 0 else 0
                        phi = min(M, TA + 1 - 128 * s)
                        if t == 0 and s == 0:
                            plo = 0
                        if t == NTA - 1 and s == 3:
                            phi = M
                        jlo = TA * t + 128 * s + plo - 1
                        n = phi - plo
                        for half in range(2):
                            bi = poc[0] % 2
                            poc[0] += 1
                            for el in range(8):
                                P.add("pe", lambda e, bi=bi, el=el, s=s, M=M, half=half: e.matmul(
                                    po[bi][0:M, :], lhsT=ybuf[:, el, 128 * s:128 * s + M],
                                    rhs=WO[:, el, half * 512:(half + 1) * 512], start=(el == 0), stop=(el == 7)),
                                    reads=[("ybuf", el), ("WO", el, half)], writes=[("po", bi)])
                            P.add("dve", lambda e, bi=bi, sl=sl, M=M, half=half: e.tensor_tensor(
                                out=xr[sl][0:M, half * 512:(half + 1) * 512], in0=po[bi][0:M, :],
                                in1=xr[sl][0:M, half * 512:(half + 1) * 512], op=ALU.add),
                                reads=[("po", bi), ("xr", sl)], writes=[("xr", sl)])
                        r0 = TA * t + 128 * s
                        n1 = (n // 16) * 16
                        for pi_, (a_, b_) in enumerate(((plo, plo + n1), (plo + n1, plo + n))):
                            if b_ > a_:
                                P.add("pool", lambda e, sl=sl, a_=a_, b_=b_, r0=r0: e.dma_start(
                                    out=(H1a if pa == 0 else H1)[r0 + a_:r0 + b_, :], in_=xr[sl][a_:b_, :]),
                                    reads=[("xr", sl)], writes=[("H1", t, s, pi_)], dma=("h1w", sl, pi_))

                if pa == 0 and DIRECT_CAST:
                    load_w0(0)
                    load_w0(1)
                prep_load(0)
                prep_T(0)
                for t in range(NTA):
                    xr_dma(t)
                    if t + 1 < NTA:
                        prep_dma(t + 1)
                    phase1(t, t + 1 if t + 1 < NTA else None)
                    if t + 1 < NTA:
                        prep_T(t + 1, (3,))
                    phase2(t)
                P.flush()

        if debug == "A":
            return nc

        with ExitStack() as st:
            def sb(name, shape, dt):
                return st.enter_context(nc.sbuf_tensor(uniq(name), list(shape), dt))

            def psum(name, shape, dt):
                return st.enter_context(nc.psum_tensor(uniq(name), list(shape), dt))

            W1 = sb("W1", [128, 8, 6144], BF16)
            wst = [sb("wstb%d" % i, [128, 1024], F32) for i in range(6)]
            gb = sb("gb1", [128, DM], F32)
            xa = [sb("xab%d" % i, [128, DM], F32) for i in range(4)]
            xn = [sb("xnb%d" % i, [128, DM], BF16) for i in range(4)]
            ss = [sb("ssb%d" % i, [128, 1], F32) for i in range(4)]
            vv = [sb("vvb%d" % i, [128, 1], F32) for i in range(4)]
            rs = [sb("rsb%d" % i, [128, 1], F32) for i in range(4)]
            hnTs = [sb("hnT%d" % i, [128, 8, 512], BF16) for i in range(2)]
            cur = {"h": 0}
            sq = [sb("sq%d" % i, [128, 512], BF16) for i in range(2)]
            lnv = [sb("lnv%d" % i, [128, 512], F32) for i in range(2)]
            rq = [sb("rq%d" % i, [128, 512], F32) for i in range(2)]
            qa = [sb("qa%d" % i, [128, 512], BF16) for i in range(2)]
            qb = [sb("qb%d" % i, [128, 512], BF16) for i in range(2)]
            kk = [sb("kk%d" % i, [128, 512], BF16) for i in range(2)]
            e1 = [sb("e1%d" % i, [128, 512], F32) for i in range(2)]
            zo = [sb("zo%d" % i, [128, 512], BF16) for i in range(2)]
            vo = [sb("vo%d" % i, [128, 512], BF16) for i in range(2)]
            pj = [psum("pjb%d" % i, [128, 512], F32) for i in range(5)]
            pn = [psum("pnb%d" % i, [128, 512], F32) for i in range(2)]
            pT = psum("pTb", [128, 8, 128], BF16)

            P.add("sp", lambda e: e.dma_start(out=gb[:], in_=ng[1:2, :].partition_broadcast(128)),
                  writes=["gb"], dma="gb")
            col_order = []
            for g in (2, 0, 1):
                col_order += [g * 1536 + 512 + hp * 128 for hp in range(4)]
                col_order += [g * 1536 + 1024 + i * 128 for i in range(4)]
            for g in range(3):
                for hp in range(4):
                    col_order += [g * 1536 + hp * 128, 4608 + g * 512 + hp * 128]
            assert len(set(col_order)) == 48
            wloaded = set()
            wn = [0]

            def ensure_w(c0):
                if c0 in wloaded:
                    return
                wloaded.add(c0)
                n = wn[0]
                wn[0] += 1
                P.add("sp", lambda e, c0=c0: e.dma_start(out=W1[:, :, c0:c0 + 128], in_=W1S[c0 // 128]),
                      writes=[("W1", c0 // 128)], dma=("w1d", n % 4))

            scnt = 0
            pjc = [0]
            pnc = [0]
            cnt = {"q": 0, "k": 0, "z": 0, "v": 0}

            NPJB = 5

            def proj(c0):
                ensure_w(c0)
                bi = pjc[0] % NPJB
                pjc[0] += 1
                for k in range(8):
                    hi = cur["h"]
                    P.add("pe", lambda e, bi=bi, k=k, c0=c0, hi=hi: e.matmul(pj[bi][:, :], lhsT=W1[:, k, c0:c0 + 128],
                                                                             rhs=hnTs[hi][:, k, :], start=(k == 0), stop=(k == 7)),
                          reads=[("W1", c0 // 128), ("hnT", hi)], writes=[("pj", bi)])
                return bi

            def item_qk(kind, c, c0, pos):
                st = {}

                def s1():
                    bi = proj(c0)
                    ni = pnc[0] % 2
                    pnc[0] += 1
                    st["bi"], st["ni"] = bi, ni
                    P.add("act", lambda e: e.activation(out=sq[ni][:, :], in_=pj[bi][:, :], func=AF.Square),
                          reads=[("pj", bi)], writes=[("sq", ni)])

                def s2():
                    bi, ni = st["bi"], st["ni"]
                    P.add("pe", lambda e: e.matmul(pn[ni][:, :], lhsT=bones[:], rhs=sq[ni][:, :], start=True, stop=True),
                          reads=["bones", ("sq", ni)], writes=[("pn", ni)])
                    P.add("act", lambda e: e.activation(out=lnv[ni][:, :], in_=pn[ni][:, :], func=AF.Ln, bias=EPS),
                          reads=[("pn", ni)], writes=[("lnv", ni)])
                    P.add("act", lambda e: e.activation(out=rq[ni][:, :], in_=lnv[ni][:, :], func=AF.Exp, scale=-0.5),
                          reads=[("lnv", ni)], writes=[("rq", ni)])
                    if kind == "q":
                        qi = cnt["q"] % 2
                        cnt["q"] += 1
                        P.add("dve", lambda e: e.scalar_tensor_tensor(
                            out=qa[qi][:, :], in0=pj[bi][:, :], scalar=gqA[:, c:c + 1], in1=rq[ni][:, :],
                            op0=ALU.mult, op1=ALU.mult),
                            reads=[("pj", bi), ("rq", ni), "gqA"], writes=[("qa", qi)])
                        P.add("dve", lambda e: e.scalar_tensor_tensor(
                            out=qb[qi][:, :], in0=pj[bi][:, :], scalar=gqB[:, c:c + 1], in1=rq[ni][:, :],
                            op0=ALU.mult, op1=ALU.mult),
                            reads=[("pj", bi), ("rq", ni), "gqB"], writes=[("qb", qi)])
                        P.add("pool", lambda e: e.dma_start(out=QS[c, 0, :, pos:pos + 512], in_=qa[qi][:, :]),
                              reads=[("qa", qi)], writes=[("QSa", c, pos)], dma=("qa", qi))
                        P.add("pool", lambda e: e.dma_start(out=QS[c, 1, :, pos:pos + 512], in_=qb[qi][:, :]),
                              reads=[("qb", qi)], writes=[("QSb", c, pos)], dma=("qb", qi))
                    else:
                        ki = cnt["k"] % 2
                        cnt["k"] += 1
                        P.add("dve", lambda e: e.scalar_tensor_tensor(
                            out=kk[ki][:, :], in0=pj[bi][:, :], scalar=gkS[:, c:c + 1], in1=rq[ni][:, :],
                            op0=ALU.mult, op1=ALU.mult),
                            reads=[("pj", bi), ("rq", ni), "gkS"], writes=[("kk", ki)])
                        P.add("pool", lambda e: e.dma_start(out=KS[c, :, pos:pos + 512], in_=kk[ki][:, :]),
                              reads=[("kk", ki)], writes=[("KS", c, pos)], dma=("kk", ki))
                return s1, s2

            def item_z(c, c0, pos):
                st = {}

                def s1():
                    bi = proj(c0)
                    zi = cnt["z"] % 2
                    cnt["z"] += 1
                    st["bi"], st["zi"] = bi, zi
                    P.add("act", lambda e: e.activation(out=e1[zi][:, :], in_=pj[bi][:, :], func=AF.Exp, scale=-1.0),
                          reads=[("pj", bi)], writes=[("e1", zi)])

                def s2():
                    bi, zi = st["bi"], st["zi"]
                    P.add("act", lambda e: e.activation(out=e1[zi][:, :], in_=e1[zi][:, :], func=AF.Ln, bias=1.0),
                          reads=[("e1", zi)], writes=[("e1", zi)])
                    P.add("act", lambda e: e.activation(out=e1[zi][:, :], in_=e1[zi][:, :], func=AF.Exp, scale=-1.0),
                          reads=[("e1", zi)], writes=[("e1", zi)])
                    P.add("dve", lambda e: e.tensor_tensor(out=zo[zi][:, :], in0=pj[bi][:, :], in1=e1[zi][:, :], op=ALU.mult),
                          reads=[("pj", bi), ("e1", zi)], writes=[("zo", zi)])
                    P.add("pool", lambda e: e.dma_start(out=ZS[c, :, pos:pos + 512], in_=zo[zi][:, :]),
                          reads=[("zo", zi)], writes=[("ZS", c, pos)], dma=("zo", zi))
                return s1, s2

            def item_v(g, s_, r0):
                st = {}

                def s1():
                    bi = pjc[0] % NPJB
                    pjc[0] += 1
                    st["bi"] = bi
                    vc0 = g * 1536 + 1024
                    for i_ in range(4):
                        ensure_w(vc0 + 128 * i_)
                    hi = cur["h"]
                    for k in range(8):
                        P.add("pe", lambda e, k=k, hi=hi: e.matmul(
                            pj[bi][:, :], lhsT=hnTs[hi][:, k, 128 * s_:128 * s_ + 128], rhs=W1[:, k, vc0:vc0 + 512],
                            start=(k == 0), stop=(k == 7)),
                            reads=[("hnT", hi)] + [("W1", vc0 // 128 + i) for i in range(4)], writes=[("pj", bi)])

                def s2():
                    bi = st["bi"]
                    vi = cnt["v"] % 2
                    cnt["v"] += 1
                    if vi:
                        P.add("act", lambda e: e.activation(out=vo[vi][:, :], in_=pj[bi][:, :], func=AF.Copy),
                              reads=[("pj", bi)], writes=[("vo", vi)])
                    else:
                        P.add("dve", lambda e: e.tensor_copy(out=vo[vi][:, :], in_=pj[bi][:, :]),
                              reads=[("pj", bi)], writes=[("vo", vi)])
                    P.add("pool", lambda e: e.dma_start(
                        out=VS[4 * g:4 * g + 4, r0:r0 + 128, :].rearrange("h t c -> t h c"),
                        in_=vo[vi][:, :].rearrange("t (h c) -> t h c", h=4)),
                        reads=[("vo", vi)], writes=[("VS", g, r0)], dma=("vo", vi))
                return s1, s2

            def prep_dma_B(tt):
                j0 = 512 * tt
                for s in range(4):
                    r0 = j0 + 128 * s
                    P.add("sp", lambda e, s=s, r0=r0: e.dma_start(out=xa[s][:, :], in_=H1[r0 + 1:r0 + 129, :]),
                          writes=[("xa", s)], dma=("xa", s))

            def prep_sub_B(tt, s):
                sl = s
                P.add("act", lambda e, sl=sl, s=s: e.activation(out=xn[s][:, :], in_=xa[sl][:, :], func=AF.Square,
                                                               accum_out=ss[sl][:, :]),
                      reads=[("xa", sl)], writes=[("xn", s), ("ss", sl)])
                P.add("dve", lambda e, sl=sl: e.tensor_scalar(out=vv[sl][:, :], in0=ss[sl][:, :], scalar1=1.0 / DM,
                                                             scalar2=EPS, op0=ALU.mult, op1=ALU.add),
                      reads=[("ss", sl)], writes=[("vv", sl)])
                P.add("pool", lambda e, sl=sl: e.tensor_tensor(out=rs[sl][:, :], in0=vv[sl][:, :], in1=mh[:, :], op=ALU.pow),
                      reads=[("vv", sl), "mh"], writes=[("rs", sl)])
                P.add("dve", lambda e, sl=sl, s=s: e.scalar_tensor_tensor(
                    out=xn[s][:, :], in0=xa[sl][:, :], scalar=rs[sl][:, :], in1=gb[:, :], op0=ALU.mult, op1=ALU.mult),
                    reads=[("xa", sl), ("rs", sl), "gb"], writes=[("xn", s)])

            def prep_load_B(tt):
                prep_dma_B(tt)
                for s in range(4):
                    prep_sub_B(tt, s)

            def prep_T_B(tt, subs=(0, 1, 2, 3)):
                hi = tt % 2
                for s in subs:
                    for k in range(8):
                        P.add("pe", lambda e, s=s, k=k: e.transpose(out=pT[:, k, :], in_=xn[s][:, k * 128:(k + 1) * 128],
                                                                    identity=ident[:, :]),
                              reads=[("xn", s), "ident"], writes=["pT"])
                    if s % 2:
                        P.add("act", lambda e, s=s, hi=hi: e.activation(out=hnTs[hi][:, :, 128 * s:128 * s + 128], in_=pT[:, :, :], func=AF.Copy),
                              reads=["pT"], writes=[("hnT", hi)])
                    else:
                        P.add("dve", lambda e, s=s, hi=hi: e.tensor_copy(out=hnTs[hi][:, :, 128 * s:128 * s + 128], in_=pT[:, :, :]),
                              reads=["pT"], writes=[("hnT", hi)])

            pending = []
            pf = []
            for g in range(3):
                for hp in range(4):
                    pf += [g * 1536 + hp * 128, 4608 + g * 512 + hp * 128]
            prep_load_B(0)
            prep_T_B(0)
            for tt in range(12):
                j0 = 512 * tt
                own = 2 <= tt < 10
                kv01 = 1 <= tt <= 10
                cur["h"] = tt % 2
                if tt + 1 < 12:
                    prep_dma_B(tt + 1)
                items = []
                for g in range(3):
                    for hp in range(4):
                        c = g * 4 + hp
                        if own:
                            items.append(item_qk("q", c, g * 1536 + hp * 128, (tt - 2) * 512))
                        if g == 2 or kv01:
                            items.append(item_qk("k", c, g * 1536 + 512 + hp * 128, j0))
                        if own:
                            items.append(item_z(c, 4608 + g * 512 + hp * 128, (tt - 2) * 512))
                for g in range(3):
                    if g == 2 or kv01:
                        for s in range(4):
                            items.append(item_v(g, s, j0 + 128 * s))
                for n, (s1, s2) in enumerate(items):
                    s1()
                    if tt >= 1 and pf:
                        ensure_w(pf.pop(0))
                    for p2 in pending:
                        p2()
                    pending = [s2]
                    if tt + 1 < 12:
                        L_ = len(items)
                        for s_ in range(4):
                            if n == (s_ + 1) * L_ // 10:
                                prep_sub_B(tt + 1, s_)
                        sp_ = max(1, (L_ - L_ // 2 - 1) // 4)
                        for s_ in range(4):
                            if n == L_ // 2 + sp_ * s_:
                                prep_T_B(tt + 1, (s_,))
            for s2 in pending:
                s2()
            P.flush()

        if debug == "B":
            return nc

        with ExitStack() as st:
            def sb(name, shape, dt):
                return st.enter_context(nc.sbuf_tensor(uniq(name), list(shape), dt))

            def psum(name, shape, dt):
                return st.enter_context(nc.psum_tensor(uniq(name), list(shape), dt))

            QA = [sb("QA%d" % i, [128, OWN], BF16) for i in range(2)]
            QB = [sb("QB%d" % i, [128, OWN], BF16) for i in range(2)]
            KT = [sb("KT%d" % i, [128, REG], BF16) for i in range(2)]
            ZT = [sb("ZT%d" % i, [128, OWN], BF16) for i in range(2)]
            Vb = sb("Vb", [128, 48, 2, 128], BF16)
            Dn = sb("Dn", [128, OWN], F32)
            U = [sb("U%d" % i, [128, OWN], F32) for i in range(3)]
            yb = [sb("yb%d" % i, [128, OWN], BF16) for i in range(2)]
            pex = [sb("pex%d" % i, [128, 2, 2, 128], BF16) for i in range(4)]
            pm = [sb("pm%d" % i, [128, 2, 2, 128], BF16) for i in range(4)]
            pS = [psum("pS%d" % i, [128, 2, 2, 128], F32) for i in range(4)]
            pO = [psum("pO%d" % i, [128, 128], F32) for i in range(2)]
            pD = [psum("pD%d" % i, [128, 128], F32) for i in range(2)]

            P.add("pool", lambda e: e.memset(Vb[:], 0.0), writes=["Vbz"])
            cc = 0
            bc = 0
            yc = 0
            for hp in range(4):
                for g in range(3):
                    c = g * 4 + hp
                    d = DIL[g]
                    Lk = OWN + 128 * d
                    reg0 = HALO - 64 * d
                    nti = 32 // d + 1
                    nb = 32 // d
                    st_ = cc % 2
                    cc += 1
                    P.add("sp", lambda e, st_=st_, c=c: e.dma_start(out=QA[st_][:, :], in_=QS[c, 0, :, :]),
                          writes=[("QA", st_)], dma=("QA", st_))
                    P.add("sp", lambda e, st_=st_, c=c: e.dma_start(out=QB[st_][:, :], in_=QS[c, 1, :, :]),
                          writes=[("QB", st_)], dma=("QB", st_))
                    P.add("sp", lambda e, st_=st_, c=c, reg0=reg0, Lk=Lk: e.dma_start(out=KT[st_][:, 0:Lk], in_=KS[c, :, reg0:reg0 + Lk]),
                          writes=[("KT", st_)], dma=("KT", st_))
                    P.add("sp", lambda e, st_=st_, c=c: e.dma_start(out=ZT[st_][:, :], in_=ZS[c, :, :]),
                          writes=[("ZT", st_)], dma=("ZT", st_))
                    nv = 0
                    for r in range(d):
                        for i0 in range(0, nti, 8):
                            ni_ = min(8, nti - i0)
                            for hh in range(2):
                                off = (c * REG + reg0 + r + d * 128 * i0) * 128 + 64 * hh
                                src = bass.AP(VS_h, off, [[d * 128, 128], [128 * d * 128, ni_], [1, 64]])
                                n0 = r * nti + i0
                                P.add("sp", lambda e, src=src, n0=n0, ni_=ni_, hh=hh: e.dma_start(
                                    out=Vb[:, n0:n0 + ni_, hh, 64 * hh:64 * hh + 64], in_=src),
                                    reads=["Vbz"], writes=[("Vb", hh, n_) for n_ in range(n0, n0 + ni_)], dma=("Vb", nv % 12))
                                nv += 1
                    def stageS(r, b, bi, st_=st_, d=d, nti=nti, g=g, hp=hp):
                        q0 = r + 128 * b * d
                        qsl = slice(q0, q0 + 127 * d + 1, d) if d > 1 else slice(q0, q0 + 128)
                        for hh in range(2):
                            Qt = QA if hh == 0 else QB
                            for i in range(2):
                                k0 = r + 128 * (b + i) * d
                                ksl = slice(k0, k0 + 127 * d + 1, d) if d > 1 else slice(k0, k0 + 128)
                                P.add("pe", lambda e, hh=hh, i=i, ksl=ksl, Qt=Qt: e.matmul(
                                    pS[bi][:, hh, i, :], lhsT=KT[st_][:, ksl], rhs=Qt[st_][:, qsl], start=True, stop=True),
                                    reads=[("KT", st_), ("QA", st_), ("QB", st_)], writes=[("pS", bi)])
                        for i in range(2):
                            tn = r * nti + b + i
                            P.add("act", lambda e, i=i, tn=tn: e.activation(out=pex[bi][:, :, i, :], in_=pS[bi][:, :, i, :], func=AF.Exp,
                                                                            bias=NB[:, g, tn:tn + 1]),
                                  reads=[("pS", bi), "NB"], writes=[("pex", bi, i)])
                        meng = "pool" if bi == 3 else "dve"
                        P.add(meng, lambda e: e.tensor_tensor(
                            out=pm[bi][:], in0=pex[bi][:], in1=EB4[:, 8 * g + 2 * hp:8 * g + 2 * hp + 2, :, :], op=ALU.mult),
                            reads=[("pex", bi, 0), ("pex", bi, 1), "EB"], writes=[("pm", bi, 0), ("pm", bi, 1)])

                    def stageO(r, b, bi, oi, st_=st_, d=d, nti=nti, g=g):
                        q0 = r + 128 * b * d
                        qsl = slice(q0, q0 + 127 * d + 1, d) if d > 1 else slice(q0, q0 + 128)
                        n_ = 0
                        for hh in range(2):
                            for i in range(2):
                                tn = r * nti + b + i
                                P.add("pe", lambda e, hh=hh, i=i, tn=tn, n_=n_: e.matmul(
                                    pO[oi][:, :], lhsT=Vb[:, tn, hh, :], rhs=pm[bi][:, hh, i, :],
                                    start=(n_ == 0), stop=(n_ == 3)),
                                    reads=[("Vb", hh, tn), ("pm", bi, i)], writes=[("pO", oi)])
                                n_ += 1
                        n_ = 0
                        for hh in range(2):
                            for i in range(2):
                                P.add("pe", lambda e, hh=hh, i=i, n_=n_: e.matmul(
                                    pD[oi][:, :], lhsT=ones2[:, hh, :], rhs=pm[bi][:, hh, i, :],
                                    start=(n_ == 0), stop=(n_ == 3)),
                                    reads=["ones2", ("pm", bi, i)], writes=[("pD", oi)])
                                n_ += 1
                        P.add("dve", lambda e: e.tensor_tensor(
                            out=U[g][:, qsl], in0=pO[oi][:, :], in1=ZT[st_][:, qsl], op=ALU.mult),
                            reads=[("pO", oi), ("ZT", st_)], writes=[("U", g)])
                        if g == 0:
                            P.add("act", lambda e: e.activation(out=Dn[:, qsl], in_=pD[oi][:, :], func=AF.Copy),
                                  reads=[("pD", oi)], writes=["Dn"])
                        else:
                            P.add("dve", lambda e: e.tensor_tensor(
                                out=Dn[:, qsl], in0=pD[oi][:, :], in1=Dn[:, qsl], op=ALU.add),
                                reads=[("pD", oi), "Dn"], writes=["Dn"])

                    blocks = [(r, b) for r in range(d) for b in range(nb)]
                    inflight = []
                    for blk in blocks + [None, None, None]:
                        if blk is not None:
                            bi = bc % 4
                            oi = bc % 2
                            bc += 1
                            stageS(blk[0], blk[1], bi)
                            inflight.append((blk[0], blk[1], bi, oi))
                        if len(inflight) > 3 or (blk is None and inflight):
                            stageO(*inflight.pop(0))
                    if g == 2:
                        P.add("act", lambda e: e.activation(out=Dn[:, :], in_=Dn[:, :], func=AF.Ln), reads=["Dn"], writes=["Dn"])
                        P.add("act", lambda e: e.activation(out=Dn[:, :], in_=Dn[:, :], func=AF.Exp, scale=-1.0),
                              reads=["Dn"], writes=["Dn"])
                        for g2 in range(3):
                            yi = yc % 2
                            yc += 1
                            c2 = g2 * 4 + hp
                            P.add("dve" if g2 != 1 else "pool", lambda e, g2=g2, yi=yi: e.tensor_tensor(
                                out=yb[yi][:, :], in0=U[g2][:, :], in1=Dn[:, :], op=ALU.mult),
                                reads=[("U", g2), "Dn"], writes=[("yb", yi)])
                            P.add("pool", lambda e, yi=yi, c2=c2: e.dma_start(out=YS[c2, :, :], in_=yb[yi][:, :]),
                                  reads=[("yb", yi)], writes=[("YS", c2)], dma=("yb", yi))
            P.flush()

        if debug == "C":
            return nc

        with ExitStack() as st:
            def sb(name, shape, dt):
                return st.enter_context(nc.sbuf_tensor(uniq(name), list(shape), dt))

            def psum(name, shape, dt):
                return st.enter_context(nc.psum_tensor(uniq(name), list(shape), dt))

            WO1 = sb("WO1", [128, 12, DM], BF16)
            wst = [sb("wstd%d" % i, [128, DM], F32) for i in range(4)]
            yT = [sb("yT%d" % i, [128, 12, 512], BF16) for i in range(2)]
            h1s = [sb("h1s%d" % i, [128, DM], F32) for i in range(3)]
            po = [psum("pod%d" % i, [128, 512], F32) for i in range(3)]
            for c in range(12):
                P.add("sp", lambda e, c=c: e.dma_start(out=WO1[:, c, :], in_=WO1S[c]),
                      writes=[("WO1", c)], dma=("wo1d", c % 4))
            hc = 0
            pc = 0
            for tt in range(8):
                yi = tt % 2
                P.add("sp", lambda e, yi=yi, tt=tt: e.dma_start(
                    out=yT[yi][:, :, :], in_=YS[:, :, 512 * tt:512 * tt + 512].rearrange("c p t -> p c t")),
                    writes=[("yT", yi)], dma=("yT", yi))
                for s in range(4):
                    hi = hc % 3
                    hc += 1
                    r0 = HALO + 512 * tt + 128 * s
                    P.add("sp", lambda e, hi=hi, r0=r0: e.dma_start(out=h1s[hi][:, :], in_=H1[r0 + 1:r0 + 129, :]),
                          writes=[("h1s", hi)], dma=("h1s", hi))
                    for half in range(2):
                        bi = pc % 3
                        pc += 1
                        for c in range(12):
                            P.add("pe", lambda e, bi=bi, c=c, yi=yi, s=s, half=half: e.matmul(
                                po[bi][:, :], lhsT=yT[yi][:, c, 128 * s:128 * s + 128],
                                rhs=WO1[:, c, half * 512:(half + 1) * 512], start=(c == 0), stop=(c == 11)),
                                reads=[("yT", yi), ("WO1", c)], writes=[("po", bi)])
                        P.add("dve", lambda e, bi=bi, hi=hi, half=half: e.tensor_tensor(
                            out=h1s[hi][:, half * 512:(half + 1) * 512], in0=po[bi][:, :],
                            in1=h1s[hi][:, half * 512:(half + 1) * 512], op=ALU.add),
                            reads=[("po", bi), ("h1s", hi)], writes=[("h1s", hi)])
                    o0 = 512 * tt + 128 * s
                    P.add("pool", lambda e, hi=hi, o0=o0: e.dma_start(out=outp[o0:o0 + 128, :], in_=h1s[hi][:, :]),
                          reads=[("h1s", hi)], dma=("outw", hi))
            P.flush()
    return nc


def make_in_maps(x, norm_g, conv_w_in, conv_kernel, conv_bias, conv_w_out, attn_w_in,
                 q_norm_g, k_norm_g, attn_w_out, rel_bias_table):
    consts = host_consts()
    f32 = np.float32
    shared = {
        "norm_g": np.ascontiguousarray(norm_g, f32),
        "conv_w_in": np.ascontiguousarray(conv_w_in[0], f32),
        "cw": np.ascontiguousarray(conv_kernel[0].reshape(3, 16, 128).transpose(2, 1, 0), f32),
        "cb": np.ascontiguousarray(conv_bias[0].reshape(16, 128).T, f32),
        "conv_w_out": np.ascontiguousarray(conv_w_out[0], f32),
        "attn_w_in": np.ascontiguousarray(attn_w_in[0], f32),
        "gqT": np.ascontiguousarray(q_norm_g[0].reshape(12, 128).T, f32),
        "gkT": np.ascontiguousarray(k_norm_g[0].reshape(12, 128).T, f32),
        "attn_w_out": np.ascontiguousarray(attn_w_out[0], f32),
        "rel_bias_table": np.ascontiguousarray(rel_bias_table, f32),
    }
    shared.update(consts)
    in_maps = []
    for c in range(NCORES):
        b, q = divmod(c, 4)
        own0 = q * OWN
        xs = np.zeros((XS_ROWS, DM), f32)
        p0 = own0 - HALO - 1
        lo, hi = max(0, p0), min(S_LEN, p0 + XS_ROWS)
        xs[lo - p0:hi - p0] = x[b, lo:hi]
        fl = np.zeros((128, 3, 48), f32)
        for g, d in enumerate(DIL):
            nti = 32 // d + 1
            reg0 = HALO - 64 * d
            t = np.arange(128)[:, None, None]
            r = np.arange(d)[None, :, None]
            i = np.arange(nti)[None, None, :]
            j = reg0 + r + d * (128 * i + t)
            p = own0 - HALO + j
            fl[:, g, :d * nti] = ((p >= 0) & (p < S_LEN)).reshape(128, d * nti).astype(f32)
        m = dict(shared)
        m["xs"] = xs
        m["fl"] = fl
        in_maps.append(m)
    return in_maps


def kernel(x, norm_g, conv_w_in, conv_kernel, conv_bias, conv_w_out, attn_w_in,
           q_norm_g, k_norm_g, attn_w_out, rel_bias_table):
    args = [np.asarray(a) for a in (x, norm_g, conv_w_in, conv_kernel, conv_bias, conv_w_out, attn_w_in,
                                    q_norm_g, k_norm_g, attn_w_out, rel_bias_table)]
    in_maps = make_in_maps(*args)
    nc = build()
    res = run_bass_kernel_spmd(nc, in_maps, core_ids=list(range(NCORES)))
    out = np.empty((2, S_LEN, DM), np.float32)
    for c in range(NCORES):
        b, q = divmod(c, 4)
        out[b, q * OWN:(q + 1) * OWN] = res.results[c]["out"]
    return out
```
